# Optimizing a Trainium2 kernel written in Bass

```python
import jax, jax.numpy as jnp
from jax import lax
import numpy as np

D_MODEL = 2048
BATCH = 16
SEQ = 2048
DEPTH = 4
DEC_BATCH = 16
DEC_SEQ = 64
PAST_LEN = 2048

CHUNK = 64
PAST_CHUNKS = 8
BAND_PAST = PAST_CHUNKS * CHUNK
BAND = BAND_PAST + CHUNK
D_CONV = D_MODEL // 2
CONV_WIDTH = 31
D_ATTN = D_MODEL // 2
HEAD_DIM = 64
N_HEADS = D_ATTN // HEAD_DIM
REL_CLIP = 128
N_REL = 2 * REL_CLIP + 1
D_FF = ((8 * D_MODEL // 3 + 255) // 256) * 256
D_IN = 2 * D_CONV + 3 * D_ATTN
EPS = 1e-6
NEG = -1e30

kernel_name = "streaming_conformer_conv_chunkattn_gated_hybrid"


def rms_norm(x, g):
    xf = x.astype(jnp.float32)
    y = xf * lax.rsqrt(jnp.mean(xf * xf, axis=-1, keepdims=True) + EPS)
    return (y * g.astype(jnp.float32)).astype(x.dtype)


def layer_norm(x, g, b):
    xf = x.astype(jnp.float32)
    mu = jnp.mean(xf, axis=-1, keepdims=True)
    var = jnp.mean(jnp.square(xf - mu), axis=-1, keepdims=True)
    y = (xf - mu) * lax.rsqrt(var + EPS)
    return (y * g.astype(jnp.float32) + b.astype(jnp.float32)).astype(x.dtype)


def mixer_inputs(x, g_norm, w_in):
    h = rms_norm(x, g_norm)
    B, T, _ = x.shape
    proj = h @ w_in
    a, b, q, k, v = jnp.split(proj, [D_CONV, 2 * D_CONV, 2 * D_CONV + D_ATTN, 2 * D_CONV + 2 * D_ATTN], axis=-1)
    u = a * jax.nn.sigmoid(b)
    heads = lambda t: t.reshape(B, T, N_HEADS, HEAD_DIM)
    return h, u, heads(q), heads(k), heads(v)


def conv_module(u, prefix, conv_w, conv_b, ln_g, ln_b, w_conv_out):
    u_ext = jnp.concatenate([prefix, u], axis=1)
    y = lax.conv_general_dilated(u_ext, conv_w[:, None, :], window_strides=(1,), padding='VALID',
                                 dimension_numbers=('NWC', 'WIO', 'NWC'),
                                 feature_group_count=D_CONV) + conv_b
    y = jax.nn.silu(layer_norm(y, ln_g, ln_b))
    return y @ w_conv_out, u_ext[:, -(CONV_WIDTH - 1):]


def chunk_attend(qc, kb, vb, q_pos, k_pos, rel_bias):
    s = jnp.einsum('bqhd,bkhd->bhqk', qc, kb).astype(jnp.float32) * (HEAD_DIM ** -0.5)
    rel = jnp.clip(q_pos[:, None] - k_pos[None, :], -REL_CLIP, REL_CLIP) + REL_CLIP
    s = s + rel_bias[:, rel].astype(jnp.float32)[None]
    qch = q_pos // CHUNK
    kch = k_pos // CHUNK
    mask = (k_pos[None, :] >= 0) & (kch[None, :] <= qch[:, None]) & (kch[None, :] >= qch[:, None] - PAST_CHUNKS)
    s = jnp.where(mask[None, None], s, NEG)
    p = jax.nn.softmax(s, axis=-1).astype(vb.dtype)
    return jnp.einsum('bhqk,bkhd->bqhd', p, vb)


def attn_prompt(q, k, v, rel_bias):
    B, T, H, hd = q.shape
    n_chunks = T // CHUNK
    pad = ((0, 0), (BAND_PAST, 0), (0, 0), (0, 0))
    k_pad = jnp.pad(k, pad)
    v_pad = jnp.pad(v, pad)

    def one_chunk(c):
        start = c * CHUNK
        qc = lax.dynamic_slice_in_dim(q, start, CHUNK, axis=1)
        kb = lax.dynamic_slice_in_dim(k_pad, start, BAND, axis=1)
        vb = lax.dynamic_slice_in_dim(v_pad, start, BAND, axis=1)
        q_pos = start + jnp.arange(CHUNK)
        k_pos = start - BAND_PAST + jnp.arange(BAND)
        return chunk_attend(qc, kb, vb, q_pos, k_pos, rel_bias)

    out = lax.map(one_chunk, jnp.arange(n_chunks))
    return jnp.moveaxis(out, 0, 1).reshape(B, T, H, hd)


def attn_sample(q, k, v, cache_k, cache_v, rel_bias):
    T = q.shape[1]
    keep = cache_k.shape[1]
    k_all = jnp.concatenate([cache_k, k], axis=1)
    v_all = jnp.concatenate([cache_v, v], axis=1)
    q_pos = PAST_LEN + jnp.arange(T)
    k_pos = PAST_LEN - keep + jnp.arange(keep + T)
    out = chunk_attend(q, k_all, v_all, q_pos, k_pos, rel_bias)
    return out, k_all[:, -keep:], v_all[:, -keep:]


def merge(h, y_conv, y_attn, w_o, w_gate, b_gate, w_out):
    B, T, _ = h.shape
    y_attn = y_attn.reshape(B, T, D_ATTN) @ w_o
    g = jax.nn.sigmoid(h @ w_gate + b_gate)
    g_conv, g_attn = jnp.split(g, 2, axis=-1)
    return (g_conv * y_conv + g_attn * y_attn) @ w_out


def ffn(x, g_norm, w_up, w_down):
    h = rms_norm(x, g_norm)
    gate, up = jnp.split(h @ w_up, 2, axis=-1)
    return (jax.nn.silu(gate) * up) @ w_down


def setup_inputs(seed: int = 0) -> dict:
    key = jax.random.key(seed)
    ks = jax.random.split(key, 24)
    keep = min(BAND_PAST, PAST_LEN)
    f = jnp.float32
    nrm = lambda k, shape, scale: jax.random.normal(k, shape, f) * scale
    gain = lambda k, shape: 1.0 + 0.05 * jax.random.normal(k, shape, f)
    return {
        "x_prompt": nrm(ks[0], (BATCH, SEQ, D_MODEL), 1.0),
        "x_sample": nrm(ks[1], (DEC_BATCH, DEC_SEQ, D_MODEL), 1.0),
        "cache_k": nrm(ks[2], (DEPTH, DEC_BATCH, keep, N_HEADS, HEAD_DIM), 1.0),
        "cache_v": nrm(ks[3], (DEPTH, DEC_BATCH, keep, N_HEADS, HEAD_DIM), 1.0),
        "state_conv": nrm(ks[4], (DEPTH, DEC_BATCH, CONV_WIDTH - 1, D_CONV), 0.5),
        "norm_mix": gain(ks[5], (DEPTH, D_MODEL)),
        "w_in": nrm(ks[6], (DEPTH, D_MODEL, D_IN), D_MODEL ** -0.5),
        "conv_w": nrm(ks[7], (DEPTH, CONV_WIDTH, D_CONV), CONV_WIDTH ** -0.5),
        "conv_b": nrm(ks[8], (DEPTH, D_CONV), 0.02),
        "ln_g": gain(ks[9], (DEPTH, D_CONV)),
        "ln_b": nrm(ks[10], (DEPTH, D_CONV), 0.02),
        "w_conv_out": nrm(ks[11], (DEPTH, D_CONV, D_MODEL), D_CONV ** -0.5),
        "rel_bias": nrm(ks[12], (DEPTH, N_HEADS, N_REL), 0.5),
        "w_o": nrm(ks[13], (DEPTH, D_ATTN, D_MODEL), D_ATTN ** -0.5),
        "w_gate": nrm(ks[14], (DEPTH, D_MODEL, 2 * D_MODEL), D_MODEL ** -0.5),
        "b_gate": nrm(ks[15], (DEPTH, 2 * D_MODEL), 0.1),
        "w_out": nrm(ks[16], (DEPTH, D_MODEL, D_MODEL), D_MODEL ** -0.5),
        "norm_ffn": gain(ks[17], (DEPTH, D_MODEL)),
        "w_up": nrm(ks[18], (DEPTH, D_MODEL, 2 * D_FF), D_MODEL ** -0.5),
        "w_down": nrm(ks[19], (DEPTH, D_FF, D_MODEL), D_FF ** -0.5),
        "norm_final": gain(ks[20], (D_MODEL,)),
    }


def reference(x_prompt, x_sample, cache_k, cache_v, state_conv, norm_mix, w_in, conv_w, conv_b,
              ln_g, ln_b, w_conv_out, rel_bias, w_o, w_gate, b_gate, w_out, norm_ffn, w_up,
              w_down, norm_final):
    xp, xs = x_prompt, x_sample
    B, T = xp.shape[0], xp.shape[1]
    keep_p = min(BAND_PAST, T)
    kp, vp, cp, ksl, vsl, csl = [], [], [], [], [], []
    for l in range(DEPTH):
        h, u, q, k, v = mixer_inputs(xp, norm_mix[l], w_in[l])
        zero_prefix = jnp.zeros((B, CONV_WIDTH - 1, D_CONV), u.dtype)
        y_conv, conv_state = conv_module(u, zero_prefix, conv_w[l], conv_b[l], ln_g[l], ln_b[l], w_conv_out[l])
        y_attn = attn_prompt(q, k, v, rel_bias[l])
        xp = xp + merge(h, y_conv, y_attn, w_o[l], w_gate[l], b_gate[l], w_out[l])
        xp = xp + ffn(xp, norm_ffn[l], w_up[l], w_down[l])
        kp.append(k[:, -keep_p:])
        vp.append(v[:, -keep_p:])
        cp.append(conv_state)
        h, u, q, k, v = mixer_inputs(xs, norm_mix[l], w_in[l])
        y_conv, conv_state = conv_module(u, state_conv[l], conv_w[l], conv_b[l], ln_g[l], ln_b[l], w_conv_out[l])
        y_attn, k_buf, v_buf = attn_sample(q, k, v, cache_k[l], cache_v[l], rel_bias[l])
        xs = xs + merge(h, y_conv, y_attn, w_o[l], w_gate[l], b_gate[l], w_out[l])
        xs = xs + ffn(xs, norm_ffn[l], w_up[l], w_down[l])
        ksl.append(k_buf)
        vsl.append(v_buf)
        csl.append(conv_state)
    y_prompt = rms_norm(xp, norm_final)
    y_sample = rms_norm(xs, norm_final)
    return (y_prompt, y_sample, jnp.stack(kp), jnp.stack(vp), jnp.stack(cp),
            jnp.stack(ksl), jnp.stack(vsl), jnp.stack(csl))
```

```python
import numpy as np
import concourse.bass as bass
import concourse.mybir as mybir
from concourse.bass_utils import run_bass_kernel_spmd

F32 = mybir.dt.float32
BF16 = mybir.dt.bfloat16
AF = mybir.ActivationFunctionType
ALU = mybir.AluOpType

D = 2048
NCH = 16
DEPTH = 4
SEQ = 2048
DEC = 64
DC = 1024
DFF = 5632
NFF = 44
CW = 31
NH = 16
EPS = 1e-6
SB_BASE = 16512
SB_END = 229376
GRAN = 512


class V:
    __slots__ = ("ap", "regs")

    def __init__(self, ap, regs):
        self.ap = ap
        self.regs = regs


class Buf:
    def __init__(self, th, space, base, esize, is_dram, root_ap=None, track=True):
        self.track = track
        self.th = th
        self.space = space
        self.base = base
        self.esize = esize
        self.is_dram = is_dram
        self.root = root_ap if root_ap is not None else th
        if not is_dram:
            n = 1
            for s in th.shape[1:]:
                n *= s
            self.pstep = n

    def view(self, ap):
        pairs = list(ap.ap)
        off = ap.offset
        if not self.is_dram:
            pairs = pairs[1:]
            off = off % self.pstep
        if not self.track:
            return V(ap, [])
        ext = 0
        for s, c in pairs:
            ext += (c - 1) * abs(s)
        lo = self.base + off * self.esize
        hi = lo + (ext + 1) * self.esize
        if self.space[0] == "P":
            lo = (lo // 2048) * 2048
            hi = ((hi + 2047) // 2048) * 2048
        return V(ap, [(self.space, lo, hi)])

    def __getitem__(self, idx):
        return self.view(self.root[idx])


class Op:
    __slots__ = ("eng", "fn", "waits", "signaled", "idx", "dma_sem", "dma_val")


class Sched:
    ENGS = ("pe", "act", "dve", "pool", "sp")

    def __init__(self, nc):
        self.nc = nc
        self.ops = {e: [] for e in self.ENGS}
        self.lastw = {}
        self.reads = {}
        self.known = {e: {} for e in self.ENGS}
        self.nchan = {"sp": 8, "pool": 6, "act": 2}
        self.chan_cnt = {q: [0] * n for q, n in self.nchan.items()}
        self.chan_next = {q: 0 for q in self.nchan}
        self.sb_off = SB_BASE
        self.n_sb = 0

    def sb(self, name, shape, dtype, at=None):
        es = 4 if dtype == F32 else 2
        n = 1
        for s in shape:
            n *= s
        nbytes = ((n * es + 31) // 32) * 32
        if at is None:
            at = self.sb_off
            self.sb_off += nbytes
        assert at % 32 == 0 and at + nbytes <= SB_END, (name, at, nbytes)
        th = self.nc.alloc_sbuf_tensor_at(name, [128] + list(shape), dtype, offset=at)
        b = Buf(th, "S", at, es, False)
        b.at = at
        b.nbytes = nbytes
        return b

    def ps(self, name, ncols):
        th = self.nc.alloc_psum_tensor(name, [128, ncols], F32)
        return Buf(th, "P:" + name, 0, 4, False)

    def dram(self, name, shape, kind, dtype=F32):
        t = self.nc.dram_tensor(name, list(shape), dtype, kind=kind)
        return Buf(t, "D:" + name, 0, 1, True, root_ap=t.ap(), track=(kind != "ExternalInput"))

    def _grans(self, reg):
        sp, lo, hi = reg
        g = GRAN if sp[0] != "D" else (1 << 22)
        return [(sp, i) for i in range(lo // g, (hi - 1) // g + 1)]

    def _deps(self, reads, writes):
        deps = set()
        for v in reads:
            for reg in v.regs:
                _, lo, hi = reg
                for g in self._grans(reg):
                    for (wl, wh, ev) in self.lastw.get(g, ()):
                        if wl < hi and lo < wh:
                            deps.add(ev)
        for v in writes:
            for reg in v.regs:
                _, lo, hi = reg
                for g in self._grans(reg):
                    for (wl, wh, ev) in self.lastw.get(g, ()):
                        if wl < hi and lo < wh:
                            deps.add(ev)
                    for (rl, rh, _e, ev) in self.reads.get(g, ()):
                        if rl < hi and lo < rh:
                            deps.add(ev)
        return deps

    def _record(self, eng, ev, reads, writes):
        for v in reads:
            for reg in v.regs:
                _, lo, hi = reg
                for g in self._grans(reg):
                    lst = self.reads.setdefault(g, [])
                    for i, r in enumerate(lst):
                        if r[0] == lo and r[1] == hi and r[2] == eng:
                            lst[i] = (lo, hi, eng, ev)
                            break
                    else:
                        lst.append((lo, hi, eng, ev))
        for v in writes:
            for reg in v.regs:
                sp, lo, hi = reg
                gs = GRAN if sp[0] != "D" else (1 << 22)
                for g in self._grans(reg):
                    glo, ghi = g[1] * gs, (g[1] + 1) * gs
                    keep = []
                    for w in self.lastw.get(g, ()):
                        if not (lo <= max(w[0], glo) and min(w[1], ghi) <= hi):
                            keep.append(w)
                    keep.append((lo, hi, ev))
                    self.lastw[g] = keep
                    rl = self.reads.get(g)
                    if rl:
                        self.reads[g] = [r for r in rl if not (lo <= max(r[0], glo) and min(r[1], ghi) <= hi)]

    def _add(self, eng, fn, reads, writes, dma=False):
        op = Op()
        op.eng = eng
        op.fn = fn
        op.idx = len(self.ops[eng])
        op.signaled = False
        op.dma_sem = None
        deps = self._deps(reads, writes)
        waits_e = {}
        waits_d = {}
        if dma:
            c = self.chan_next[eng]
            self.chan_next[eng] = (c + 1) % self.nchan[eng]
            prev = self.chan_cnt[eng][c]
            if prev:
                waits_d[(eng, c)] = prev * 16
            self.chan_cnt[eng][c] = prev + 1
            op.dma_sem = (eng, c)
            op.dma_val = (prev + 1) * 16
            ev = ("D", (eng, c), op.dma_val)
        else:
            ev = ("E", eng, op.idx)
        for d in deps:
            if d[0] == "E":
                if d[1] == eng and eng == "pe":
                    continue
                waits_e[d[1]] = max(waits_e.get(d[1], -1), d[2])
            else:
                waits_d[d[1]] = max(waits_d.get(d[1], 0), d[2])
        kn = self.known[eng]
        op.waits = []
        for src, idx in waits_e.items():
            if kn.get(("E", src), -1) >= idx:
                continue
            kn[("E", src)] = idx
            self.ops[src][idx].signaled = True
            op.waits.append(("E", src, idx))
        for ch, val in waits_d.items():
            if kn.get(("D", ch), 0) >= val:
                continue
            kn[("D", ch)] = val
            op.waits.append(("D", ch, val))
        self.ops[eng].append(op)
        self._record(eng if not dma else (eng, op.dma_sem[1]), ev, reads, writes)
        return op

    def op(self, eng, fn, reads, writes):
        return self._add(eng, fn, reads, writes)

    def dma(self, q, out, in_, **kw):
        return self._add(q, lambda e: e.dma_start(out=out.ap, in_=in_.ap, **kw), [in_], [out], dma=True)

    def emit(self):
        nc = self.nc
        counts = {}
        for e in self.ENGS:
            c = 0
            arr = []
            for op in self.ops[e]:
                if op.signaled:
                    c += 1
                arr.append(c)
            counts[e] = arr
        import contextlib
        with contextlib.ExitStack() as st:
            esem = {e: st.enter_context(nc.semaphore("se_" + e)) for e in self.ENGS}
            dsem = {}
            for q, n in self.nchan.items():
                for c in range(n):
                    dsem[(q, c)] = st.enter_context(nc.semaphore("sd_%s%d" % (q, c)))
            block = st.enter_context(nc.Block())

            def run(eng_name, handle, final=False):
                for op in self.ops[eng_name]:
                    for w in op.waits:
                        if w[0] == "E":
                            handle.wait_ge(esem[w[1]], counts[w[1]][w[2]])
                        else:
                            handle.wait_ge(dsem[w[1]], w[2])
                    ins = op.fn(handle)
                    if op.dma_sem is not None:
                        ins.then_inc(dsem[op.dma_sem], 16)
                    elif op.signaled:
                        ins.then_inc(esem[eng_name], 1)
                if final:
                    for q, n in self.nchan.items():
                        for c in range(n):
                            if self.chan_cnt[q][c]:
                                handle.wait_ge(dsem[(q, c)], self.chan_cnt[q][c] * 16)

            @block.sync
            def _(e):
                run("sp", e, final=True)

            @block.gpsimd
            def _(e):
                run("pool", e)

            @block.scalar
            def _(e):
                run("act", e)

            @block.vector
            def _(e):
                run("dve", e)

            @block.tensor
            def _(e):
                run("pe", e)


class TileDesc:
    pass


class _Stop(Exception):
    pass


def build(NL, do_final=True, stop=None):
    def ckpt(name):
        if stop == name:
            raise _Stop()
    nc = bass.Bass("TRN2", target_bir_lowering=False)
    S = Sched(nc)
    IN = "ExternalInput"
    OUT = "ExternalOutput"
    xp = S.dram("xp", [2, SEQ, D], IN)
    xsm = S.dram("xsm", [2, DEC, D], IN)
    ck = S.dram("ck", [NL, 2, 512, DC], IN)
    cv = S.dram("cv", [NL, 2, 512, DC], IN)
    scv = S.dram("scv", [NL, 2, 30, DC], IN)
    norm_mix = S.dram("norm_mix", [NL, D], IN)
    w_in = S.dram("w_in", [NL, D, 5120], IN)
    conv_w = S.dram("conv_w", [NL, CW, DC], IN)
    conv_b = S.dram("conv_b", [NL, DC], IN)
    ln_g = S.dram("ln_g", [NL, DC], IN)
    ln_b = S.dram("ln_b", [NL, DC], IN)
    w_co = S.dram("w_conv_out", [NL, DC, D], IN)
    rel_bias = S.dram("rel_bias", [NL, NH, 257], IN)
    w_o = S.dram("w_o", [NL, DC, D], IN)
    w_gate = S.dram("w_gate", [NL, D, 2 * D], IN)
    b_gate = S.dram("b_gate", [NL, 2 * D], IN)
    w_out = S.dram("w_out", [NL, D, D], IN)
    norm_ffn = S.dram("norm_ffn", [NL, D], IN)
    w_up = S.dram("w_up", [NL, D, 2 * DFF], IN)
    w_down = S.dram("w_down", [NL, DFF, D], IN)
    norm_final = S.dram("norm_final", [1, D], IN)
    cst = S.dram("cst", [128, 3, 128], IN)
    maskc = S.dram("maskc", [128, 4, 128], IN)
    yp = S.dram("yp", [2, SEQ, D], OUT)
    ysm = S.dram("ysm", [2, DEC, D], OUT)
    nkp = S.dram("nkp", [NL, 2, 512, DC], OUT)
    nvp = S.dram("nvp", [NL, 2, 512, DC], OUT)
    ncp = S.dram("ncp", [NL, 2, 30, DC], OUT)
    nks = S.dram("nks", [NL, 2, 512, DC], OUT)
    nvs = S.dram("nvs", [NL, 2, 512, DC], OUT)
    ncs = S.dram("ncs", [NL, 2, 30, DC], OUT)
    xscr = S.dram("xscr", [9, 128, NCH, 512], "Internal")
    ext = S.dram("ext", [NH, 388], "Internal")
    c16 = S.dram("c16", [1, NH], "Internal")
    NBLK = 136
    wscrs = [S.dram("wscr%d" % i, [NBLK, 128, 4096], "Internal", dtype=BF16) for i in range(2)]
    wstate = {"b": 0, "first": True, "l": 0, "tile": 0}

    xT = S.sb("xT", [NCH, 512], F32)
    hT = S.sb("hT", [NCH, 512], BF16)
    zone = S.sb_off
    uT = S.sb("uT", [8, 544], F32)
    yc = S.sb("yc", [8, 512], F32)
    qT = S.sb("qT", [8, 512], BF16)
    ya = S.sb("ya", [1024], F32)
    S.sb_off = max(S.sb_off, zone + NFF * 512 * 2)
    fT = S.sb("fT", [NFF, 512], BF16, at=zone)
    ycT = S.sb("ycT", [8, 512], BF16, at=uT.at)
    mT = S.sb("mT", [NCH, 512], BF16, at=yc.at)
    xstage = S.sb("xstage", [D], F32, at=yc.at)
    kstage = S.sb("kstage", [DC], F32, at=yc.at + 8192)
    pstage = S.sb("pstage", [DC], F32, at=yc.at + 12288)
    yaT = S.sb("yaT", [8, 512], BF16)
    a_tm = S.sb("a_tm", [DC], F32, at=yaT.at)
    sigb_tm = S.sb("sigb_tm", [256], F32, at=yaT.at + 4096)
    kring = S.sb("kring", [8, 1280], BF16)
    vring = S.sb("vring", [10, NH, 65], BF16)
    biasT = S.sb("biasT", [NH, 4, 128], BF16)
    PT = [S.sb("PT%d" % i, [640], BF16) for i in range(2)]
    rstd = S.sb("rstd", [512], F32)
    sq = [S.sb("sq%d" % i, [512], BF16) for i in range(2)]
    tf = [S.sb("tf%d" % i, [512], F32) for i in range(4)]
    rcp = S.sb("rcp", [8], F32)
    tmk = [S.sb("tmk%d" % i, [256], F32, at=yaT.at + 5120 + 1024 * i) for i in range(2)]
    tmv = [S.sb("tmv0", [256], F32, at=yaT.at + 7168), S.sb("tmv1", [256], F32)]
    bstage = [S.sb("bstage%d" % i, [2, 128], F32) for i in range(2)]
    cbp = S.sb("cbp", [NH], F32)
    cb16 = S.sb("cb16", [1], F32)
    cb16x = S.sb("cb16x", [128], F32)
    vec = S.sb("vec", [336], F32)
    vecf = S.sb("vecf", [NCH], F32)
    vstage = S.sb("vstage", [3, 128], F32)
    cst_sb = S.sb("cst_sb", [3, 128], F32)
    Jb = S.sb("Jb", [128], BF16)
    onesb = S.sb("onesb", [128], BF16)
    mask_sb = S.sb("mask_sb", [4, 128], F32)
    epsb = S.sb("epsb", [1], F32)
    upref = S.sb("upref", [8, 32], F32)
    NSLOT = 3
    wsl = [S.sb("wsl%d" % i, [4096], BF16) for i in range(NSLOT)]
    ident = cst_sb[:, 0, :]
    onesf = cst_sb[:, 2, :]

    mmb = [S.ps("mm%d" % i, 512) for i in range(2)]
    stps = [S.ps("st%d" % i, 1024) for i in range(2)]
    pvps = [S.ps("pv%d" % i, 512) for i in range(2)]
    cnt = {"mm": 0, "w": 0, "pt": 0, "pv": 0, "tr": 0, "sq": 0, "tf": 0, "tmk": 0, "tmv": 0, "bs": 0, "st": 0}

    def rot(key, n):
        i = cnt[key] % n
        cnt[key] += 1
        return i

    def acc():
        return mmb[rot("mm", 2)]

    def mm(out, lhsT, rhs, start, stop):
        S.op("pe", lambda e: e.matmul(out.ap, lhsT=lhsT.ap, rhs=rhs.ap, start=start, stop=stop),
             [lhsT, rhs], [out])

    def transpose(out, in_, idn):
        S.op("pe", lambda e: e.transpose(out.ap, in_.ap, idn.ap), [in_, idn], [out])

    def act(out, in_, func, bias=None, scale=None):
        reads = [in_]
        kw = {}
        if bias is not None:
            kw["bias"] = bias.ap
            reads.append(bias)
        if scale is not None:
            if isinstance(scale, V):
                kw["scale"] = scale.ap
                reads.append(scale)
            else:
                kw["scale"] = scale
        S.op("act", lambda e: e.activation(out=out.ap, in_=in_.ap, func=func, **kw), reads, [out])

    def vcopy(eng, out, in_):
        if eng == "act":
            S.op("act", lambda e: e.copy(out=out.ap, in_=in_.ap), [in_], [out])
        else:
            S.op(eng, lambda e: e.tensor_copy(out=out.ap, in_=in_.ap), [in_], [out])

    def tt(out, in0, in1, op, eng="dve"):
        S.op(eng, lambda e: e.tensor_tensor(out=out.ap, in0=in0.ap, in1=in1.ap, op=op), [in0, in1], [out])

    def ts(out, in0, s1, s2, op0, op1=None, eng="dve"):
        reads = [in0]
        a1 = s1
        a2 = s2
        if isinstance(s1, V):
            reads.append(s1)
            a1 = s1.ap
        if isinstance(s2, V):
            reads.append(s2)
            a2 = s2.ap
        if op1 is None:
            S.op(eng, lambda e: e.tensor_scalar(out=out.ap, in0=in0.ap, scalar1=a1, scalar2=None, op0=op0), reads, [out])
        else:
            S.op(eng, lambda e: e.tensor_scalar(out=out.ap, in0=in0.ap, scalar1=a1, scalar2=a2, op0=op0, op1=op1), reads, [out])

    def stt(out, in0, scalar, in1, op0, op1, eng="dve"):
        reads = [in0, in1]
        a = scalar
        if isinstance(scalar, V):
            reads.append(scalar)
            a = scalar.ap
        S.op(eng, lambda e: e.scalar_tensor_tensor(out=out.ap, in0=in0.ap, scalar=a, in1=in1.ap, op0=op0, op1=op1), reads, [out])

    def memset(eng, out, val):
        S.op(eng, lambda e: e.memset(out.ap, val), [], [out])

    def wload(srcf, nk, ncols):
        sl = wsl[rot("w", NSLOT)]
        b = wstate["b"]
        wstate["b"] += 1
        assert b < NBLK
        l = wstate["l"]
        par = l % 2
        n = nk * ncols
        if l == 0 and wstate["first"]:
            src2d = srcf(0)
            dst = sl.view(sl.th[:, 0:n].rearrange("p (k n) -> p k n", k=nk))
            srcv = V(src2d.ap.rearrange("(k p) n -> p k n", p=128), src2d.regs)
            S.dma("pool", dst, srcv)
            S.dma("sp", wscrs[par][b, :, 0:n], sl[:, 0:n])
        else:
            S.dma("pool", sl[:, 0:n], wscrs[par][b, :, 0:n])
        if l + 1 < NL and wstate["tile"] < 8 and (b % 8) == wstate["tile"]:
            srcn = srcf(l + 1)
            wn = wscrs[1 - par]
            dstn = wn.view(wn.root[b, :, 0:n].rearrange("p (k n) -> p k n", k=nk))
            S.dma("pool", dstn, V(srcn.ap.rearrange("(k p) n -> p k n", p=128), srcn.regs))
        return sl, None

    def wv(sl, nk, ncols, k, c0, c1):
        ap = sl.th[:, 0:nk * ncols].rearrange("p (k n) -> p k n", k=nk)[:, k, c0:c1]
        return sl.view(ap)

    S.dma("sp", cst_sb[:, :, :], cst[:, :, :])
    S.dma("sp", mask_sb[:, :, :], maskc[:, :, :])
    vcopy("dve", Jb[:, :], cst_sb[:, 1, :])
    vcopy("dve", onesb[:, :], cst_sb[:, 2, :])
    memset("dve", epsb[:, :], EPS)
    memset("dve", kring[:, :, :], 0.0)
    memset("dve", vring[:, :, :, :], 0.0)
    memset("dve", vring[:, :, :, 64:65], 1.0)
    memset("dve", uT[:, :, :], 0.0)
    S.dma("sp", vstage[0:16, 0, :], S_view_rows(norm_final, 0, 16))
    trp = acc()
    transpose(trp[:, 0:16], vstage[0:16, 0, :], cst_sb[0:16, 0, 0:16])
    vcopy("dve", vecf[:, :], trp[:, 0:16])

    def rmsnorm(gcol, T, pre=None):
        if pre is None:
            ps = acc()
            for c in range(NCH):
                s = sq[rot("sq", 2)]
                act(s[:, 0:T], xT[:, c, 0:T], AF.Square)
                mm(ps[:, 0:T], onesb[:, :], s[:, 0:T], c == 0, c == NCH - 1)
        else:
            ps = pre
        act(rstd[:, 0:T], ps[:, 0:T], AF.Sqrt, bias=epsb[:, 0:1], scale=1.0 / D)
        S.op("dve", lambda e: e.reciprocal(out=rstd[:, 0:T].ap, in_=rstd[:, 0:T].ap), [rstd[:, 0:T]], [rstd[:, 0:T]])
        for c in range(NCH):
            stt(hT[:, c, 0:T], xT[:, c, 0:T], vec[:, gcol + c:gcol + c + 1], rstd[:, 0:T], ALU.mult, ALU.mult)

    def layer_setup(l):
        def rows(buf2d_ap_view, r0, nr, grp):
            S.dma("sp", vstage[r0:r0 + nr, grp, :], buf2d_ap_view)
        rows(S_view_rows(norm_mix, l, 16), 0, 16, 0)
        rows(S_view_rows(norm_ffn, l, 16), 16, 16, 0)
        rows(S_view_rows(b_gate, l, 32), 32, 32, 0)
        rows(S_view_rows(conv_b, l, 8), 64, 8, 0)
        rows(S_view_rows(ln_g, l, 8), 72, 8, 0)
        rows(S_view_rows(ln_b, l, 8), 80, 8, 0)
        cwv = conv_w.view(conv_w.root[l].rearrange("j (c p) -> (j c) p", p=128))
        S.dma("sp", vstage[0:128, 1, :], conv_w.view(cwv.ap[0:128, :]))
        S.dma("sp", vstage[0:120, 2, :], conv_w.view(cwv.ap[128:248, :]))
        for grp, nr, c0 in ((0, 88, 0), (1, 128, 88), (2, 120, 216)):
            trp = acc()
            transpose(trp[:, 0:nr], vstage[0:nr, grp, :], cst_sb[0:nr, 0, 0:nr])
            vcopy("dve", vec[:, c0:c0 + nr], trp[:, 0:nr])
        S.dma("sp", ext[:, 0:257], rel_bias[l, :, :])
        S.dma("sp", cb16[0:NH, 0:1], rel_bias[l, :, 256:257], allow_slow_non_contiguous=True)
        ts(cb16x[0:NH, :], cst_sb[0:NH, 2, :], cb16[0:NH, 0:1], None, ALU.mult)
        S.dma("sp", ext[:, 257:385], cb16x[0:NH, :])
        S.dma("sp", c16.view(c16.root[0:1, :].rearrange("o (h u) -> (o h) u", u=1)), cb16[0:NH, 0:1], allow_slow_non_contiguous=True)
        S.dma("sp", cbp[:, :], V(bass.AP(tensor=c16.th, offset=0, ap=[[0, 128], [1, NH]]), c16[0:1, :].regs), allow_slow_non_contiguous=True)
        for h in range(NH):
            bs = bstage[rot("bs", 2)]
            for jj, off in ((0, 129), (1, 1)):
                src = V(bass.AP(tensor=ext.th, offset=h * 388 + off, ap=[[1, 128], [1, 128]]),
                        [("D:ext", h * 388, (h + 1) * 388)])
                S.dma("sp", bs[:, jj, :], src)
            vcopy("dve", biasT[:, h, 0, :], mask_sb[:, 0, :])
            stt(biasT[:, h, 2, :], bs[:, 0, :], cbp[:, h:h + 1], mask_sb[:, 2, :], ALU.subtract, ALU.add)
            stt(biasT[:, h, 3, :], bs[:, 1, :], cbp[:, h:h + 1], mask_sb[:, 3, :], ALU.subtract, ALU.add)

    def S_view_rows_unused():
        pass

    JJ = [0, 1, 1, 2, 3]

    def conv_ops(l, td):
        T = td.T
        for p in range(4):
            for j in range(CW):
                for c in (2 * p, 2 * p + 1):
                    wcol = vec[:, 88 + j * 8 + c:88 + j * 8 + c + 1]
                    for (tok0, n, ucol) in td.segs:
                        src = uT[:, c, ucol - 30 + j:ucol - 30 + j + n]
                        dst = yc[:, c, tok0:tok0 + n]
                        if j == 0:
                            ts(dst, src, wcol, vec[:, 64 + c:65 + c], ALU.mult, ALU.add)
                        else:
                            stt(dst, src, wcol, dst, ALU.mult, ALU.add)
                        yield

    def attention(l, td, filler):
        units = td.units
        sched = []
        for u in units:
            for h in range(NH):
                sched.append((u, h))
        state = {}

        def emit_S(u, h):
            nq, qcol, blks, yrow = u
            c = h // 2
            p0 = (h % 2) * 64
            stp = stps[rot("st", 2)]
            sbase = 0
            for j, rb in blks:
                o = sbase + j * 128
                nb_ = JJ[j] == 1
                mm(stp[:, o:o + nq], kring[p0:p0 + 64, c, rb * 128:rb * 128 + 128], qT[p0:p0 + 64, c, qcol:qcol + nq], True, nb_)
                if not nb_:
                    mm(stp[:, o:o + nq], Jb[:, :], biasT[:, h, JJ[j], 0:nq], False, True)
            j0 = blks[0][0]
            nb = len(blks)
            pt = PT[rot("pt", 2)]
            src = stp.view(stp.th[:, sbase + j0 * 128:sbase + (j0 + nb) * 128].rearrange("p (j q) -> p j q", q=128)[:, :, 0:nq])
            dst = pt.view(pt.th[:, j0 * 128:(j0 + nb) * 128].rearrange("p (j q) -> p j q", q=128)[:, :, 0:nq])
            act(dst, src, AF.Exp)
            state[(id(u), h)] = pt

        def emit_PV(u, h):
            nq, qcol, blks, yrow = u
            pt = state.pop((id(u), h))
            pi = rot("pv", 2)
            pvp = pvps[pi]
            po = 0
            for i, (j, rb) in enumerate(blks):
                mm(pvp[0:nq, po:po + 65], pt[:, j * 128:j * 128 + nq], vring[:, rb, h, :], i == 0, i == len(blks) - 1)
            r = rcp[0:nq, pi:pi + 1]
            S.op("dve", lambda e: e.reciprocal(out=r.ap, in_=pvp[0:nq, po + 64:po + 65].ap), [pvp[0:nq, po + 64:po + 65]], [r])
            act(ya[0:nq, h * 64:(h + 1) * 64], pvp[0:nq, po:po + 64], AF.Copy, scale=r)
            for _ in range(4):
                next(filler, None)
            if h == NH - 1:
                for b4 in range(2):
                    trp = acc()
                    for cc in range(4):
                        c = 4 * b4 + cc
                        transpose(trp[:, cc * 128:cc * 128 + nq], ya[0:nq, c * 128:(c + 1) * 128], cst_sb[0:nq, 0, 0:nq])
                    src = trp.view(trp.th[:, :].rearrange("p (c q) -> p c q", q=128)[:, :, 0:nq])
                    vcopy("act", yaT[:, 4 * b4:4 * b4 + 4, qcol:qcol + nq], src)

        for i, (u, h) in enumerate(sched):
            emit_S(u, h)
            if i >= 1:
                emit_PV(*sched[i - 1])
        emit_PV(*sched[-1])
        for _ in filler:
            pass

    def tile(l, td, nxt=None):
        T = td.T
        last_layer = (l == NL - 1)
        if l == 0:
            for (src_rows, n, col0) in td.xin:
                S.dma("sp", xstage[0:n, :], src_rows)
                for b4 in range(4):
                    trp = acc()
                    for cc in range(4):
                        c = 4 * b4 + cc
                        transpose(trp[:, cc * 128:cc * 128 + n], xstage[0:n, c * 128:(c + 1) * 128], cst_sb[0:n, 0, 0:n])
                    src = trp.view(trp.th[:, :].rearrange("p (c q) -> p c q", q=128)[:, :, 0:n])
                    vcopy("act" if b4 % 2 else "dve", xT[:, 4 * b4:4 * b4 + 4, col0:col0 + n], src)
        elif not getattr(td, "prefetched", False):
            for g in range(4):
                S.dma("sp", xT[:, 4 * g:4 * g + 4, 0:T], xscr[td.tid, :, 4 * g:4 * g + 4, 0:T])
        ckpt('s0')
        rmsnorm(0, T)
        ckpt('s1')
        if td.kind == "prompt":
            if td.ti == 0:
                memset("dve", uT[:, :, 0:32], 0.0)
            else:
                vcopy("dve", uT[:, :, 0:32], upref[:, :, :])
        else:
            for s in range(2):
                S.dma("sp", pstage[0:30, :], scv[l, s, :, :])
                trp = acc()
                for c in range(8):
                    transpose(trp[:, c * 32:c * 32 + 30], pstage[0:30, c * 128:(c + 1) * 128], cst_sb[0:30, 0, 0:30])
                src = trp.view(trp.th[:, 0:256].rearrange("p (c q) -> p c q", q=32)[:, :, 0:30])
                vcopy("dve", uT[:, :, 96 * s + 2:96 * s + 32], src)
        WI = w_in
        conv_gen = conv_ops(l, td)
        for i in range(4):
            if i >= 1:
                for _ in range(16):
                    next(conv_gen, None)
            sl, _ = wload(lambda L: WI[L, :, 256 * i:256 * i + 256], NCH, 256)
            for cc in range(2):
                c = 2 * i + cc
                ps = acc()
                for k in range(NCH):
                    mm(ps[:, 0:T], wv(sl, NCH, 256, k, cc * 128, cc * 128 + 128), hT[:, k, 0:T], k == 0, k == NCH - 1)
                for (tok0, n, ucol) in td.segs:
                    vcopy("act", uT[:, c, ucol:ucol + n], ps[:, tok0:tok0 + n])
            if td.kv_out:
                col0, n = td.ugroup
                ps = acc()
                for k in range(NCH):
                    mm(ps[0:n, 0:256], hT[:, k, col0:col0 + n], wv(sl, NCH, 256, k, 0, 256), k == 0, k == NCH - 1)
                vcopy("act", a_tm[0:n, 256 * i:256 * i + 256], ps[0:n, 0:256])
            sl, _ = wload(lambda L: WI[L, :, 1024 + 256 * i:1024 + 256 * i + 256], NCH, 256)
            for cc in range(2):
                c = 2 * i + cc
                ps = acc()
                for k in range(NCH):
                    mm(ps[:, 0:T], wv(sl, NCH, 256, k, cc * 128, cc * 128 + 128), hT[:, k, 0:T], k == 0, k == NCH - 1)
                t = tf[rot("tf", 4)]
                act(t[:, 0:T], ps[:, 0:T], AF.Sigmoid)
                for (tok0, n, ucol) in td.segs:
                    tt(uT[:, c, ucol:ucol + n], uT[:, c, ucol:ucol + n], t[:, tok0:tok0 + n], ALU.mult)
            if td.kv_out:
                col0, n = td.ugroup
                ps = acc()
                for k in range(NCH):
                    mm(ps[0:n, 0:256], hT[:, k, col0:col0 + n], wv(sl, NCH, 256, k, 0, 256), k == 0, k == NCH - 1)
                act(sigb_tm[0:n, :], ps[0:n, 0:256], AF.Sigmoid)
                tt(a_tm[0:n, 256 * i:256 * i + 256], a_tm[0:n, 256 * i:256 * i + 256], sigb_tm[0:n, :], ALU.mult)
        if td.kv_out:
            for (r0, dst) in td.uout:
                S.dma("sp", dst(l), a_tm[r0:r0 + 30, :])
        if td.kind == "prompt" and td.ti < 3:
            vcopy("dve", upref[:, :, :], uT[:, :, 512:544])
        ckpt('s2ab')
        for i in range(4):
            sl, _ = wload(lambda L: WI[L, :, 2048 + 256 * i:2048 + 256 * i + 256], NCH, 256)
            for cc in range(2):
                c = 2 * i + cc
                ps = acc()
                for k in range(NCH):
                    mm(ps[:, 0:T], wv(sl, NCH, 256, k, cc * 128, cc * 128 + 128), hT[:, k, 0:T], k == 0, k == NCH - 1)
                act(qT[:, c, 0:T], ps[:, 0:T], AF.Copy, scale=0.125)
                for _ in range(5):
                    next(conv_gen, None)
        for i in range(4):
            sl, _ = wload(lambda L: WI[L, :, 3072 + 256 * i:3072 + 256 * i + 256], NCH, 256)
            for cc in range(2):
                c = 2 * i + cc
                ps = acc()
                for k in range(NCH):
                    mm(ps[:, 0:T], wv(sl, NCH, 256, k, cc * 128, cc * 128 + 128), hT[:, k, 0:T], k == 0, k == NCH - 1)
                for (col0, n, rb, rcol, _o) in td.kgroups:
                    vcopy("act", kring[:, c, rb * 128 + rcol:rb * 128 + rcol + n], ps[:, col0:col0 + n])
                for _ in range(5):
                    next(conv_gen, None)
            if td.kv_out:
                for (col0, n, rb, rcol, outf) in td.kgroups:
                    ps = acc()
                    for k in range(NCH):
                        mm(ps[0:n, 0:256], hT[:, k, col0:col0 + n], wv(sl, NCH, 256, k, 0, 256), k == 0, k == NCH - 1)
                    t = tmk[rot("tmk", 2)]
                    vcopy("act", t[0:n, :], ps[0:n, 0:256])
                    S.dma("sp", outf(nkp if td.kind == "prompt" else nks, l, 256 * i), t[0:n, :])
        for i in range(4):
            sl, _ = wload(lambda L: WI[L, :, 4096 + 256 * i:4096 + 256 * i + 256], NCH, 256)
            for (col0, n, rb, rcol, outf) in td.kgroups:
                ps = acc()
                for k in range(NCH):
                    mm(ps[0:n, 0:256], hT[:, k, col0:col0 + n], wv(sl, NCH, 256, k, 0, 256), k == 0, k == NCH - 1)
                dst = vring.view(vring.th[rcol:rcol + n, rb, 4 * i:4 * i + 4, 0:64])
                src = ps.view(ps.th[0:n, 0:256].rearrange("p (h d) -> p h d", d=64))
                vcopy("act", dst, src)
                if td.kv_out:
                    t = tmv[rot("tmv", 2)]
                    vcopy("act", t[0:n, :], ps[0:n, 0:256])
                    S.dma("sp", outf(nvp if td.kind == "prompt" else nvs, l, 256 * i), t[0:n, :])
                for _ in range(3):
                    next(conv_gen, None)
        ckpt('s2')
        attention(l, td, conv_gen)
        ckpt('s3')
        ps1 = acc()
        ps2 = acc()
        for c in range(8):
            t1 = qT[:, c, 0:T]
            act(t1, yc[:, c, 0:T], AF.Copy)
            mm(ps1[:, 0:T], onesb[:, :], t1, c == 0, c == 7)
        for c in range(8):
            t2 = PT[c % 2][:, 0:T]
            act(t2, yc[:, c, 0:T], AF.Square)
            mm(ps2[:, 0:T], onesb[:, :], t2, c == 0, c == 7)
        mean = tf[rot("tf", 4)]
        var = tf[rot("tf", 4)]
        ts(mean[:, 0:T], ps1[:, 0:T], 1.0 / DC, None, ALU.mult)
        tt(var[:, 0:T], mean[:, 0:T], mean[:, 0:T], ALU.mult)
        stt(var[:, 0:T], ps2[:, 0:T], 1.0 / DC, var[:, 0:T], ALU.mult, ALU.subtract)
        act(var[:, 0:T], var[:, 0:T], AF.Sqrt, bias=epsb[:, 0:1], scale=1.0)
        S.op("dve", lambda e: e.reciprocal(out=var[:, 0:T].ap, in_=var[:, 0:T].ap), [var[:, 0:T]], [var[:, 0:T]])
        for c in range(8):
            tt(yc[:, c, 0:T], yc[:, c, 0:T], mean[:, 0:T], ALU.subtract)
            tt(yc[:, c, 0:T], yc[:, c, 0:T], var[:, 0:T], ALU.mult)
            act(ycT[:, c, 0:T], yc[:, c, 0:T], AF.Silu, bias=vec[:, 80 + c:81 + c], scale=vec[:, 72 + c:73 + c])
        ckpt('s4')
        for g in range(8):
            c0 = 256 * g
            sgc = []
            sl, _ = wload(lambda L: w_gate[L, :, c0:c0 + 256], NCH, 256)
            for cc in range(2):
                ps = acc()
                for k in range(NCH):
                    mm(ps[:, 0:T], wv(sl, NCH, 256, k, cc * 128, cc * 128 + 128), hT[:, k, 0:T], k == 0, k == NCH - 1)
                t = tf[rot("tf", 4)]
                act(t[:, 0:T], ps[:, 0:T], AF.Sigmoid, bias=vec[:, 32 + 2 * g + cc:33 + 2 * g + cc])
                sgc.append(t)
            sl, _ = wload(lambda L: w_co[L, :, c0:c0 + 256], 8, 256)
            for cc in range(2):
                ps = acc()
                for k in range(8):
                    mm(ps[:, 0:T], wv(sl, 8, 256, k, cc * 128, cc * 128 + 128), ycT[:, k, 0:T], k == 0, k == 7)
                tt(sgc[cc][:, 0:T], ps[:, 0:T], sgc[cc][:, 0:T], ALU.mult)
            sga = []
            sl, _ = wload(lambda L: w_gate[L, :, D + c0:D + c0 + 256], NCH, 256)
            for cc in range(2):
                ps = acc()
                for k in range(NCH):
                    mm(ps[:, 0:T], wv(sl, NCH, 256, k, cc * 128, cc * 128 + 128), hT[:, k, 0:T], k == 0, k == NCH - 1)
                t = tf[rot("tf", 4)]
                act(t[:, 0:T], ps[:, 0:T], AF.Sigmoid, bias=vec[:, 48 + 2 * g + cc:49 + 2 * g + cc])
                sga.append(t)
            sl, _ = wload(lambda L: w_o[L, :, c0:c0 + 256], 8, 256)
            for cc in range(2):
                ps = acc()
                for k in range(8):
                    mm(ps[:, 0:T], wv(sl, 8, 256, k, cc * 128, cc * 128 + 128), yaT[:, k, 0:T], k == 0, k == 7)
                tt(sga[cc][:, 0:T], ps[:, 0:T], sga[cc][:, 0:T], ALU.mult)
                tt(mT[:, 2 * g + cc, 0:T], sgc[cc][:, 0:T], sga[cc][:, 0:T], ALU.add)
        ckpt('s5')
        stat = pvps[0]
        pend = None
        nst = 0
        for g in range(8):
            sl, _ = wload(lambda L: w_out[L, :, 256 * g:256 * g + 256], NCH, 256)
            for cc in range(2):
                oc = 2 * g + cc
                ps = acc()
                for k in range(NCH):
                    mm(ps[:, 0:T], wv(sl, NCH, 256, k, cc * 128, cc * 128 + 128), mT[:, k, 0:T], k == 0, k == NCH - 1)
                if pend is not None:
                    S.op("pe", (lambda o, a, b, st: (lambda e: e.matmul(o.ap, lhsT=a.ap, rhs=b.ap, start=st, stop=False, skip_group_check=True)))(stat[:, 0:T], onesb[:, :], pend[:, 0:T], nst == 0),
                         [onesb[:, :], pend[:, 0:T]], [stat[:, 0:T]])
                    nst += 1
                tt(xT[:, oc, 0:T], xT[:, oc, 0:T], ps[:, 0:T], ALU.add)
                pend = sq[rot("sq", 2)]
                act(pend[:, 0:T], xT[:, oc, 0:T], AF.Square)
        S.op("pe", (lambda o, a, b: (lambda e: e.matmul(o.ap, lhsT=a.ap, rhs=b.ap, start=False, stop=True, skip_group_check=True)))(stat[:, 0:T], onesb[:, :], pend[:, 0:T]),
             [onesb[:, :], pend[:, 0:T]], [stat[:, 0:T]])
        ckpt('s6')
        rmsnorm(16, T, pre=stat)
        for g in range(NFF // 2):
            sg = []
            sl, _ = wload(lambda L: w_up[L, :, 256 * g:256 * g + 256], NCH, 256)
            for cc in range(2):
                ps = acc()
                for k in range(NCH):
                    mm(ps[:, 0:T], wv(sl, NCH, 256, k, cc * 128, cc * 128 + 128), hT[:, k, 0:T], k == 0, k == NCH - 1)
                t = tf[rot("tf", 4)]
                act(t[:, 0:T], ps[:, 0:T], AF.Silu)
                sg.append(t)
            sl, _ = wload(lambda L: w_up[L, :, DFF + 256 * g:DFF + 256 * g + 256], NCH, 256)
            for cc in range(2):
                ps = acc()
                for k in range(NCH):
                    mm(ps[:, 0:T], wv(sl, NCH, 256, k, cc * 128, cc * 128 + 128), hT[:, k, 0:T], k == 0, k == NCH - 1)
                tt(fT[:, 2 * g + cc, 0:T], ps[:, 0:T], sg[cc][:, 0:T], ALU.mult)
        for oc in range(NCH):
            ps = acc()
            for half in range(2):
                sl, _ = wload(lambda L: w_down[L, half * 2816:(half + 1) * 2816, oc * 128:(oc + 1) * 128], 22, 128)
                for k in range(22):
                    kk = half * 22 + k
                    mm(ps[:, 0:T], wv(sl, 22, 128, k, 0, 128), fT[:, kk, 0:T], kk == 0, kk == NFF - 1)
            tt(xT[:, oc, 0:T], xT[:, oc, 0:T], ps[:, 0:T], ALU.add)
            if oc % 4 == 3 and not last_layer:
                g = oc // 4
                S.dma("sp", xscr[td.tid, :, 4 * g:4 * g + 4, 0:T], xT[:, 4 * g:4 * g + 4, 0:T])
                if nxt is not None:
                    S.dma("sp", xT[:, 4 * g:4 * g + 4, 0:nxt.T], xscr[nxt.tid, :, 4 * g:4 * g + 4, 0:nxt.T])
                    nxt.prefetched = True
        ckpt('s7')
        if not last_layer:
            pass
        elif do_final:
            ps = acc()
            for c in range(NCH):
                s = sq[rot("sq", 2)]
                act(s[:, 0:T], xT[:, c, 0:T], AF.Square)
                mm(ps[:, 0:T], onesb[:, :], s[:, 0:T], c == 0, c == NCH - 1)
            act(rstd[:, 0:T], ps[:, 0:T], AF.Sqrt, bias=epsb[:, 0:1], scale=1.0 / D)
            S.op("dve", lambda e: e.reciprocal(out=rstd[:, 0:T].ap, in_=rstd[:, 0:T].ap), [rstd[:, 0:T]], [rstd[:, 0:T]])
            for (dst_rows, n, col0) in td.yout:
                for b4 in range(4):
                    trp = acc()
                    for cc in range(4):
                        c = 4 * b4 + cc
                        t = tf[rot("tf", 4)]
                        stt(t[:, 0:n], xT[:, c, col0:col0 + n], vecf[:, c:c + 1], rstd[:, col0:col0 + n], ALU.mult, ALU.mult)
                        transpose(trp[0:n, cc * 128:(cc + 1) * 128], t[:, 0:n], ident)
                    vcopy("act", xstage[0:n, b4 * 512:(b4 + 1) * 512], trp[0:n, :])
                S.dma("sp", dst_rows, xstage[0:n, :])

    def prompt_td(seq, ti):
        td = TileDesc()
        td.kind = "prompt"
        td.T = 512
        td.ti = ti
        td.tid = seq * 4 + ti
        td.segs = [(0, 512, 32)]
        td.kv_out = (ti == 3)
        td.ugroup = (384, 128)
        td.uout = [(98, lambda l: ncp[l, seq, :, :])]
        td.kgroups = []
        for tb in range(4):
            rb = (4 * ti + tb) % 8

            def outf(buf, l, c0, tb=tb):
                return buf[l, seq, tb * 128:(tb + 1) * 128, c0:c0 + 256]
            td.kgroups.append((tb * 128, 128, rb, 0, outf))
        td.units = []
        for p in range(4):
            blks = []
            for j in range(5):
                gb = 4 * ti + p - 4 + j
                if gb >= 0:
                    blks.append((j, gb % 8))
            td.units.append((128, p * 128, blks, 0))
        t0 = ti * 512
        td.xin = [(xp[seq, t0 + tb * 128:t0 + (tb + 1) * 128, :], 128, tb * 128) for tb in range(4)]
        td.yout = [(yp[seq, t0 + tb * 128:t0 + (tb + 1) * 128, :], 128, tb * 128) for tb in range(4)]
        return td

    def sample_td():
        td = TileDesc()
        td.kind = "sample"
        td.T = 128
        td.ti = 0
        td.tid = 8
        td.segs = [(0, 64, 32), (64, 64, 128)]
        td.kv_out = True
        td.ugroup = (0, 128)
        td.uout = [(34, lambda l: ncs[l, 0, :, :]), (98, lambda l: ncs[l, 1, :, :])]
        td.kgroups = []
        for s in range(2):
            def outf(buf, l, c0, s=s):
                return buf[l, s, 448:512, c0:c0 + 256]
            td.kgroups.append((s * 64, 64, 5 * s + 4, 0, outf))
        td.units = []
        for s in range(2):
            td.units.append((64, s * 64, [(j, 5 * s + j) for j in range(5)], 0))
        td.xin = [(xsm[s, :, :], 64, s * 64) for s in range(2)]
        td.yout = [(ysm[s, :, :], 64, s * 64) for s in range(2)]
        return td

    def sample_cache(l):
        for s in range(2):
            S.dma("sp", nks[l, s, 0:448, :], ck[l, s, 64:512, :])
            S.dma("sp", nvs[l, s, 0:448, :], cv[l, s, 64:512, :])
            for b in range(4):
                dst = vring.view(vring.th[:, 5 * s + b, :, 0:64])
                srcv = cv.view(cv.root[l, s, b * 128:(b + 1) * 128, :].rearrange("p (h d) -> p h d", d=64))
                S.dma("pool", dst, srcv)
            for b in range(4):
                S.dma("sp", kstage[:, :], ck[l, s, b * 128:(b + 1) * 128, :])
                for b4 in range(2):
                    trp = acc()
                    for cc in range(4):
                        c = 4 * b4 + cc
                        transpose(trp[:, cc * 128:(cc + 1) * 128], kstage[:, c * 128:(c + 1) * 128], ident)
                    src = trp.view(trp.th[:, :].rearrange("p (c q) -> p c q", q=128))
                    vcopy("act" if b4 % 2 else "dve", kring[:, 4 * b4:4 * b4 + 4, (5 * s + b) * 128:(5 * s + b + 1) * 128], src)

    tds_all = [[prompt_td(seq, ti) for seq in range(2) for ti in range(4)] + [sample_td()] for _ in range(NL)]
    try:
        ckpt("init")
        for l in range(NL):
            layer_setup(l)
            ckpt("setup")
            tds = tds_all[l]
            for i, td in enumerate(tds):
                if td.kind == "sample":
                    sample_cache(l)
                    ckpt("scache")
                wstate["b"] = 0
                wstate["first"] = (i == 0)
                wstate["l"] = l
                wstate["tile"] = i
                if i + 1 < len(tds):
                    nxt, nxt_l = tds[i + 1], l
                elif l + 1 < NL:
                    nxt, nxt_l = tds_all[l + 1][0], l + 1
                else:
                    nxt, nxt_l = None, 0
                tile(l, td, nxt if nxt_l > 0 else None)
                ckpt("tile%d" % i)
                ckpt("L%dtile%d" % (l, i))
    except _Stop:
        pass
    S.emit()
    return nc, S


def S_view_rows(buf, l, nrows):
    return buf.view(buf.root[l:l + 1, :].rearrange("o (r p) -> (o r) p", p=128))


def make_consts():
    cst = np.zeros((128, 3, 128), np.float32)
    cst[:, 0, :] = np.eye(128, dtype=np.float32)
    cst[:, 1, :] = np.eye(128, dtype=np.float32)[::-1]
    cst[:, 2, :] = 1.0
    NEG = -30000.0
    m = np.zeros((128, 4, 128), np.float32)
    for kp in range(128):
        k = 127 - kp
        for q in range(128):
            cq = q // 64
            kc = k // 64
            if (-8 + kc) < (cq - 8):
                m[kp, 0, q] = NEG
            if kc > cq:
                m[kp, 3, q] = NEG
    return cst, m


_CACHE = {}
WNAMES = ["norm_mix", "w_in", "conv_w", "conv_b", "ln_g", "ln_b", "w_conv_out", "rel_bias", "w_o", "w_gate",
          "b_gate", "w_out", "norm_ffn", "w_up", "w_down"]


def run(inputs, NL=DEPTH, cores=8, do_final=True, trace=False, stop=None):
    key = (NL, do_final, stop)
    if key not in _CACHE:
        _CACHE[key] = build(NL, do_final, stop)[0]
    nc = _CACHE[key]
    cst, m = make_consts()
    f = lambda a: np.ascontiguousarray(np.asarray(a, dtype=np.float32))
    shared = {n: f(inputs[n][:NL]) for n in WNAMES}
    shared["norm_final"] = f(inputs["norm_final"]).reshape(1, D)
    shared["cst"] = cst
    shared["maskc"] = m
    in_maps = []
    for c in range(cores):
        d = dict(shared)
        d["xp"] = f(inputs["x_prompt"][2 * c:2 * c + 2])
        d["xsm"] = f(inputs["x_sample"][2 * c:2 * c + 2])
        d["ck"] = f(inputs["cache_k"][:NL, 2 * c:2 * c + 2]).reshape(NL, 2, 512, DC)
        d["cv"] = f(inputs["cache_v"][:NL, 2 * c:2 * c + 2]).reshape(NL, 2, 512, DC)
        d["scv"] = f(inputs["state_conv"][:NL, 2 * c:2 * c + 2])
        in_maps.append(d)
    res = run_bass_kernel_spmd(nc, in_maps, core_ids=list(range(cores)), trace=trace)
    R = res.results
    cat = lambda name, ax: np.concatenate([np.asarray(r[name]) for r in R], axis=ax)
    nb = 2 * cores
    outs = (
        cat("yp", 0),
        cat("ysm", 0),
        cat("nkp", 1).reshape(NL, nb, 512, NH, 64),
        cat("nvp", 1).reshape(NL, nb, 512, NH, 64),
        cat("ncp", 1),
        cat("nks", 1).reshape(NL, nb, 512, NH, 64),
        cat("nvs", 1).reshape(NL, nb, 512, NH, 64),
        cat("ncs", 1),
    )
    return outs, res


def kernel(**inputs):
    outs, _ = run(inputs)
    return outs
```

```python
import numpy as np
import concourse.bass as bass
import concourse.mybir as mybir
from concourse.bass_utils import run_bass_kernel_spmd

F32 = mybir.dt.float32
BF16 = mybir.dt.bfloat16
AF = mybir.ActivationFunctionType
ALU = mybir.AluOpType

D = 2048
NCH = 16
DEPTH = 4
SEQ = 2048
DEC = 64
DC = 1024
DFF = 5632
NFF = 44
CW = 31
NH = 16
EPS = 1e-6
SB_BASE = 16512
SB_END = 229376
GRAN = 512


class V:
    __slots__ = ("ap", "regs")

    def __init__(self, ap, regs):
        self.ap = ap
        self.regs = regs


class Buf:
    def __init__(self, th, space, base, esize, is_dram, root_ap=None, track=True):
        self.track = track
        self.th = th
        self.space = space
        self.base = base
        self.esize = esize
        self.is_dram = is_dram
        self.root = root_ap if root_ap is not None else th
        if not is_dram:
            n = 1
            for s in th.shape[1:]:
                n *= s
            self.pstep = n

    def view(self, ap):
        pairs = list(ap.ap)
        off = ap.offset
        if not self.is_dram:
            pairs = pairs[1:]
            off = off % self.pstep
        if not self.track:
            return V(ap, [])
        ext = 0
        for s, c in pairs:
            ext += (c - 1) * abs(s)
        lo = self.base + off * self.esize
        hi = lo + (ext + 1) * self.esize
        if self.space[0] == "P":
            lo = (lo // 2048) * 2048
            hi = ((hi + 2047) // 2048) * 2048
        return V(ap, [(self.space, lo, hi)])

    def __getitem__(self, idx):
        return self.view(self.root[idx])


class Op:
    __slots__ = ("eng", "fn", "waits", "signaled", "idx", "dma_sem", "dma_val")


class Sched:
    ENGS = ("pe", "act", "dve", "pool", "sp")

    def __init__(self, nc):
        self.nc = nc
        self.ops = {e: [] for e in self.ENGS}
        self.lastw = {}
        self.reads = {}
        self.known = {e: {} for e in self.ENGS}
        self.nchan = {"sp": 8, "pool": 6, "act": 2}
        self.chan_cnt = {q: [0] * n for q, n in self.nchan.items()}
        self.chan_next = {q: 0 for q in self.nchan}
        self.sb_off = SB_BASE
        self.n_sb = 0

    def sb(self, name, shape, dtype, at=None):
        es = 4 if dtype == F32 else 2
        n = 1
        for s in shape:
            n *= s
        nbytes = ((n * es + 31) // 32) * 32
        if at is None:
            at = self.sb_off
            self.sb_off += nbytes
        assert at % 32 == 0 and at + nbytes <= SB_END, (name, at, nbytes)
        th = self.nc.alloc_sbuf_tensor_at(name, [128] + list(shape), dtype, offset=at)
        b = Buf(th, "S", at, es, False)
        b.at = at
        b.nbytes = nbytes
        return b

    def ps(self, name, ncols):
        th = self.nc.alloc_psum_tensor(name, [128, ncols], F32)
        return Buf(th, "P:" + name, 0, 4, False)

    def dram(self, name, shape, kind, dtype=F32):
        t = self.nc.dram_tensor(name, list(shape), dtype, kind=kind)
        return Buf(t, "D:" + name, 0, 1, True, root_ap=t.ap(), track=(kind != "ExternalInput"))

    def _grans(self, reg):
        sp, lo, hi = reg
        g = GRAN if sp[0] != "D" else (1 << 22)
        return [(sp, i) for i in range(lo // g, (hi - 1) // g + 1)]

    def _deps(self, reads, writes):
        deps = set()
        for v in reads:
            for reg in v.regs:
                _, lo, hi = reg
                for g in self._grans(reg):
                    for (wl, wh, ev) in self.lastw.get(g, ()):
                        if wl < hi and lo < wh:
                            deps.add(ev)
        for v in writes:
            for reg in v.regs:
                _, lo, hi = reg
                for g in self._grans(reg):
                    for (wl, wh, ev) in self.lastw.get(g, ()):
                        if wl < hi and lo < wh:
                            deps.add(ev)
                    for (rl, rh, _e, ev) in self.reads.get(g, ()):
                        if rl < hi and lo < rh:
                            deps.add(ev)
        return deps

    def _record(self, eng, ev, reads, writes):
        for v in reads:
            for reg in v.regs:
                _, lo, hi = reg
                for g in self._grans(reg):
                    lst = self.reads.setdefault(g, [])
                    for i, r in enumerate(lst):
                        if r[0] == lo and r[1] == hi and r[2] == eng:
                            lst[i] = (lo, hi, eng, ev)
                            break
                    else:
                        lst.append((lo, hi, eng, ev))
        for v in writes:
            for reg in v.regs:
                sp, lo, hi = reg
                gs = GRAN if sp[0] != "D" else (1 << 22)
                for g in self._grans(reg):
                    glo, ghi = g[1] * gs, (g[1] + 1) * gs
                    keep = []
                    for w in self.lastw.get(g, ()):
                        if not (lo <= max(w[0], glo) and min(w[1], ghi) <= hi):
                            keep.append(w)
                    keep.append((lo, hi, ev))
                    self.lastw[g] = keep
                    rl = self.reads.get(g)
                    if rl:
                        self.reads[g] = [r for r in rl if not (lo <= max(r[0], glo) and min(r[1], ghi) <= hi)]

    def _add(self, eng, fn, reads, writes, dma=False):
        op = Op()
        op.eng = eng
        op.fn = fn
        op.idx = len(self.ops[eng])
        op.signaled = False
        op.dma_sem = None
        deps = self._deps(reads, writes)
        waits_e = {}
        waits_d = {}
        if dma:
            c = self.chan_next[eng]
            self.chan_next[eng] = (c + 1) % self.nchan[eng]
            prev = self.chan_cnt[eng][c]
            if prev:
                waits_d[(eng, c)] = prev * 16
            self.chan_cnt[eng][c] = prev + 1
            op.dma_sem = (eng, c)
            op.dma_val = (prev + 1) * 16
            ev = ("D", (eng, c), op.dma_val)
        else:
            ev = ("E", eng, op.idx)
        for d in deps:
            if d[0] == "E":
                if d[1] == eng and eng == "pe":
                    continue
                waits_e[d[1]] = max(waits_e.get(d[1], -1), d[2])
            else:
                waits_d[d[1]] = max(waits_d.get(d[1], 0), d[2])
        kn = self.known[eng]
        op.waits = []
        for src, idx in waits_e.items():
            if kn.get(("E", src), -1) >= idx:
                continue
            kn[("E", src)] = idx
            self.ops[src][idx].signaled = True
            op.waits.append(("E", src, idx))
        for ch, val in waits_d.items():
            if kn.get(("D", ch), 0) >= val:
                continue
            kn[("D", ch)] = val
            op.waits.append(("D", ch, val))
        self.ops[eng].append(op)
        self._record(eng if not dma else (eng, op.dma_sem[1]), ev, reads, writes)
        return op

    def op(self, eng, fn, reads, writes):
        return self._add(eng, fn, reads, writes)

    def dma(self, q, out, in_, **kw):
        return self._add(q, lambda e: e.dma_start(out=out.ap, in_=in_.ap, **kw), [in_], [out], dma=True)

    def emit(self):
        nc = self.nc
        counts = {}
        for e in self.ENGS:
            c = 0
            arr = []
            for op in self.ops[e]:
                if op.signaled:
                    c += 1
                arr.append(c)
            counts[e] = arr
        import contextlib
        with contextlib.ExitStack() as st:
            esem = {e: st.enter_context(nc.semaphore("se_" + e)) for e in self.ENGS}
            dsem = {}
            for q, n in self.nchan.items():
                for c in range(n):
                    dsem[(q, c)] = st.enter_context(nc.semaphore("sd_%s%d" % (q, c)))
            block = st.enter_context(nc.Block())

            def run(eng_name, handle, final=False):
                for op in self.ops[eng_name]:
                    for w in op.waits:
                        if w[0] == "E":
                            handle.wait_ge(esem[w[1]], counts[w[1]][w[2]])
                        else:
                            handle.wait_ge(dsem[w[1]], w[2])
                    ins = op.fn(handle)
                    if op.dma_sem is not None:
                        ins.then_inc(dsem[op.dma_sem], 16)
                    elif op.signaled:
                        ins.then_inc(esem[eng_name], 1)
                if final:
                    for q, n in self.nchan.items():
                        for c in range(n):
                            if self.chan_cnt[q][c]:
                                handle.wait_ge(dsem[(q, c)], self.chan_cnt[q][c] * 16)

            @block.sync
            def _(e):
                run("sp", e, final=True)

            @block.gpsimd
            def _(e):
                run("pool", e)

            @block.scalar
            def _(e):
                run("act", e)

            @block.vector
            def _(e):
                run("dve", e)

            @block.tensor
            def _(e):
                run("pe", e)


class TileDesc:
    pass


class _Stop(Exception):
    pass


def build(NL, do_final=True, stop=None):
    def ckpt(name):
        if stop == name:
            raise _Stop()
    nc = bass.Bass("TRN2", target_bir_lowering=False)
    S = Sched(nc)
    IN = "ExternalInput"
    OUT = "ExternalOutput"
    xp = S.dram("xp", [2, SEQ, D], IN)
    xsm = S.dram("xsm", [2, DEC, D], IN)
    ck = S.dram("ck", [NL, 2, 512, DC], IN)
    cv = S.dram("cv", [NL, 2, 512, DC], IN)
    scv = S.dram("scv", [NL, 2, 30, DC], IN)
    norm_mix = S.dram("norm_mix", [NL, D], IN)
    w_in = S.dram("w_in", [NL, D, 5120], IN)
    conv_w = S.dram("conv_w", [NL, CW, DC], IN)
    conv_b = S.dram("conv_b", [NL, DC], IN)
    ln_g = S.dram("ln_g", [NL, DC], IN)
    ln_b = S.dram("ln_b", [NL, DC], IN)
    w_co = S.dram("w_conv_out", [NL, DC, D], IN)
    rel_bias = S.dram("rel_bias", [NL, NH, 257], IN)
    w_o = S.dram("w_o", [NL, DC, D], IN)
    w_gate = S.dram("w_gate", [NL, D, 2 * D], IN)
    b_gate = S.dram("b_gate", [NL, 2 * D], IN)
    w_out = S.dram("w_out", [NL, D, D], IN)
    norm_ffn = S.dram("norm_ffn", [NL, D], IN)
    w_up = S.dram("w_up", [NL, D, 2 * DFF], IN)
    w_down = S.dram("w_down", [NL, DFF, D], IN)
    norm_final = S.dram("norm_final", [1, D], IN)
    cst = S.dram("cst", [128, 3, 128], IN)
    maskc = S.dram("maskc", [128, 4, 128], IN)
    yp = S.dram("yp", [2, SEQ, D], OUT)
    ysm = S.dram("ysm", [2, DEC, D], OUT)
    nkp = S.dram("nkp", [NL, 2, 512, DC], OUT)
    nvp = S.dram("nvp", [NL, 2, 512, DC], OUT)
    ncp = S.dram("ncp", [NL, 2, 30, DC], OUT)
    nks = S.dram("nks", [NL, 2, 512, DC], OUT)
    nvs = S.dram("nvs", [NL, 2, 512, DC], OUT)
    ncs = S.dram("ncs", [NL, 2, 30, DC], OUT)
    xscr = S.dram("xscr", [9, 128, NCH, 512], "Internal")
    ext = S.dram("ext", [NH, 388], "Internal")
    c16 = S.dram("c16", [1, NH], "Internal")
    NBLK = 136
    wscr = S.dram("wscr", [NBLK, 128, 4096], "Internal", dtype=BF16)
    wstate = {"b": 0, "first": True}

    xT = S.sb("xT", [NCH, 512], F32)
    hT = S.sb("hT", [NCH, 512], BF16)
    zone = S.sb_off
    uT = S.sb("uT", [8, 544], F32)
    yc = S.sb("yc", [8, 512], F32)
    qT = S.sb("qT", [8, 512], BF16)
    ya = S.sb("ya", [1024], F32)
    S.sb_off = max(S.sb_off, zone + NFF * 512 * 2)
    fT = S.sb("fT", [NFF, 512], BF16, at=zone)
    ycT = S.sb("ycT", [8, 512], BF16, at=uT.at)
    mT = S.sb("mT", [NCH, 512], BF16, at=yc.at)
    xstage = S.sb("xstage", [D], F32, at=yc.at)
    kstage = S.sb("kstage", [DC], F32, at=yc.at + 8192)
    pstage = S.sb("pstage", [DC], F32, at=yc.at + 12288)
    yaT = S.sb("yaT", [8, 512], BF16)
    a_tm = S.sb("a_tm", [DC], F32, at=yaT.at)
    sigb_tm = S.sb("sigb_tm", [256], F32, at=yaT.at + 4096)
    kring = S.sb("kring", [8, 1280], BF16)
    vring = S.sb("vring", [10, NH, 65], BF16)
    biasT = S.sb("biasT", [NH, 4, 128], BF16)
    PT = [S.sb("PT%d" % i, [640], BF16) for i in range(2)]
    rstd = S.sb("rstd", [512], F32)
    sq = [S.sb("sq%d" % i, [512], BF16) for i in range(2)]
    tf = [S.sb("tf%d" % i, [512], F32) for i in range(4)]
    rcp = S.sb("rcp", [8], F32)
    tmk = [S.sb("tmk%d" % i, [256], F32, at=yaT.at + 5120 + 1024 * i) for i in range(2)]
    tmv = [S.sb("tmv0", [256], F32, at=yaT.at + 7168), S.sb("tmv1", [256], F32)]
    bstage = [S.sb("bstage%d" % i, [2, 128], F32) for i in range(2)]
    cbp = S.sb("cbp", [NH], F32)
    cb16 = S.sb("cb16", [1], F32)
    cb16x = S.sb("cb16x", [128], F32)
    vec = S.sb("vec", [336], F32)
    vecf = S.sb("vecf", [NCH], F32)
    vstage = S.sb("vstage", [3, 128], F32)
    cst_sb = S.sb("cst_sb", [3, 128], F32)
    Jb = S.sb("Jb", [128], BF16)
    onesb = S.sb("onesb", [128], BF16)
    mask_sb = S.sb("mask_sb", [4, 128], F32)
    epsb = S.sb("epsb", [1], F32)
    upref = S.sb("upref", [8, 32], F32)
    NSLOT = 3
    wsl = [S.sb("wsl%d" % i, [4096], BF16) for i in range(NSLOT)]
    ident = cst_sb[:, 0, :]
    onesf = cst_sb[:, 2, :]

    mmb = [S.ps("mm%d" % i, 512) for i in range(2)]
    stps = [S.ps("st%d" % i, 1024) for i in range(2)]
    pvps = [S.ps("pv%d" % i, 512) for i in range(2)]
    cnt = {"mm": 0, "w": 0, "pt": 0, "pv": 0, "tr": 0, "sq": 0, "tf": 0, "tmk": 0, "tmv": 0, "bs": 0, "st": 0, "mma": 0}

    def rot(key, n):
        i = cnt[key] % n
        cnt[key] += 1
        return i

    def acc():
        return (mmb + pvps)[rot("mm", 4)]

    def acc_attn():
        return mmb[rot("mma", 2)]

    def mm(out, lhsT, rhs, start, stop):
        S.op("pe", lambda e: e.matmul(out.ap, lhsT=lhsT.ap, rhs=rhs.ap, start=start, stop=stop),
             [lhsT, rhs], [out])

    def transpose(out, in_, idn):
        S.op("pe", lambda e: e.transpose(out.ap, in_.ap, idn.ap), [in_, idn], [out])

    def act(out, in_, func, bias=None, scale=None):
        reads = [in_]
        kw = {}
        if bias is not None:
            kw["bias"] = bias.ap
            reads.append(bias)
        if scale is not None:
            if isinstance(scale, V):
                kw["scale"] = scale.ap
                reads.append(scale)
            else:
                kw["scale"] = scale
        S.op("act", lambda e: e.activation(out=out.ap, in_=in_.ap, func=func, **kw), reads, [out])

    def vcopy(eng, out, in_):
        if eng == "act":
            S.op("act", lambda e: e.copy(out=out.ap, in_=in_.ap), [in_], [out])
        else:
            S.op(eng, lambda e: e.tensor_copy(out=out.ap, in_=in_.ap), [in_], [out])

    def tt(out, in0, in1, op, eng="dve"):
        S.op(eng, lambda e: e.tensor_tensor(out=out.ap, in0=in0.ap, in1=in1.ap, op=op), [in0, in1], [out])

    def ts(out, in0, s1, s2, op0, op1=None, eng="dve"):
        reads = [in0]
        a1 = s1
        a2 = s2
        if isinstance(s1, V):
            reads.append(s1)
            a1 = s1.ap
        if isinstance(s2, V):
            reads.append(s2)
            a2 = s2.ap
        if op1 is None:
            S.op(eng, lambda e: e.tensor_scalar(out=out.ap, in0=in0.ap, scalar1=a1, scalar2=None, op0=op0), reads, [out])
        else:
            S.op(eng, lambda e: e.tensor_scalar(out=out.ap, in0=in0.ap, scalar1=a1, scalar2=a2, op0=op0, op1=op1), reads, [out])

    def stt(out, in0, scalar, in1, op0, op1, eng="dve"):
        reads = [in0, in1]
        a = scalar
        if isinstance(scalar, V):
            reads.append(scalar)
            a = scalar.ap
        S.op(eng, lambda e: e.scalar_tensor_tensor(out=out.ap, in0=in0.ap, scalar=a, in1=in1.ap, op0=op0, op1=op1), reads, [out])

    def memset(eng, out, val):
        S.op(eng, lambda e: e.memset(out.ap, val), [], [out])

    def wload(src2d, nk, ncols):
        sl = wsl[rot("w", NSLOT)]
        b = wstate["b"]
        wstate["b"] += 1
        assert b < NBLK
        if wstate["first"]:
            dst = sl.view(sl.th[:, 0:nk * ncols].rearrange("p (k n) -> p k n", k=nk))
            srcv = V(src2d.ap.rearrange("(k p) n -> p k n", p=128), src2d.regs)
            S.dma("pool", dst, srcv)
            S.dma("sp", wscr[b, :, 0:nk * ncols], sl[:, 0:nk * ncols])
        else:
            S.dma("pool", sl[:, 0:nk * ncols], wscr[b, :, 0:nk * ncols])
        return sl, None

    def wv(sl, nk, ncols, k, c0, c1):
        ap = sl.th[:, 0:nk * ncols].rearrange("p (k n) -> p k n", k=nk)[:, k, c0:c1]
        return sl.view(ap)

    S.dma("sp", cst_sb[:, :, :], cst[:, :, :])
    S.dma("sp", mask_sb[:, :, :], maskc[:, :, :])
    vcopy("dve", Jb[:, :], cst_sb[:, 1, :])
    vcopy("dve", onesb[:, :], cst_sb[:, 2, :])
    memset("dve", epsb[:, :], EPS)
    memset("dve", kring[:, :, :], 0.0)
    memset("dve", vring[:, :, :, :], 0.0)
    memset("dve", vring[:, :, :, 64:65], 1.0)
    memset("dve", uT[:, :, :], 0.0)
    S.dma("sp", vstage[0:16, 0, :], S_view_rows(norm_final, 0, 16))
    trp = acc()
    transpose(trp[:, 0:16], vstage[0:16, 0, :], cst_sb[0:16, 0, 0:16])
    vcopy("dve", vecf[:, :], trp[:, 0:16])

    def rmsnorm(gcol, T):
        ps = acc()
        for c in range(NCH):
            s = sq[rot("sq", 2)]
            act(s[:, 0:T], xT[:, c, 0:T], AF.Square)
            mm(ps[:, 0:T], onesb[:, :], s[:, 0:T], c == 0, c == NCH - 1)
        act(rstd[:, 0:T], ps[:, 0:T], AF.Sqrt, bias=epsb[:, 0:1], scale=1.0 / D)
        S.op("dve", lambda e: e.reciprocal(out=rstd[:, 0:T].ap, in_=rstd[:, 0:T].ap), [rstd[:, 0:T]], [rstd[:, 0:T]])
        for c in range(NCH):
            stt(hT[:, c, 0:T], xT[:, c, 0:T], vec[:, gcol + c:gcol + c + 1], rstd[:, 0:T], ALU.mult, ALU.mult)

    def layer_setup(l):
        def rows(buf2d_ap_view, r0, nr, grp):
            S.dma("sp", vstage[r0:r0 + nr, grp, :], buf2d_ap_view)
        rows(S_view_rows(norm_mix, l, 16), 0, 16, 0)
        rows(S_view_rows(norm_ffn, l, 16), 16, 16, 0)
        rows(S_view_rows(b_gate, l, 32), 32, 32, 0)
        rows(S_view_rows(conv_b, l, 8), 64, 8, 0)
        rows(S_view_rows(ln_g, l, 8), 72, 8, 0)
        rows(S_view_rows(ln_b, l, 8), 80, 8, 0)
        cwv = conv_w.view(conv_w.root[l].rearrange("j (c p) -> (j c) p", p=128))
        S.dma("sp", vstage[0:128, 1, :], conv_w.view(cwv.ap[0:128, :]))
        S.dma("sp", vstage[0:120, 2, :], conv_w.view(cwv.ap[128:248, :]))
        for grp, nr, c0 in ((0, 88, 0), (1, 128, 88), (2, 120, 216)):
            trp = acc()
            transpose(trp[:, 0:nr], vstage[0:nr, grp, :], cst_sb[0:nr, 0, 0:nr])
            vcopy("dve", vec[:, c0:c0 + nr], trp[:, 0:nr])
        S.dma("sp", ext[:, 0:257], rel_bias[l, :, :])
        S.dma("sp", cb16[0:NH, 0:1], rel_bias[l, :, 256:257], allow_slow_non_contiguous=True)
        ts(cb16x[0:NH, :], cst_sb[0:NH, 2, :], cb16[0:NH, 0:1], None, ALU.mult)
        S.dma("sp", ext[:, 257:385], cb16x[0:NH, :])
        S.dma("sp", c16.view(c16.root[0:1, :].rearrange("o (h u) -> (o h) u", u=1)), cb16[0:NH, 0:1], allow_slow_non_contiguous=True)
        S.dma("sp", cbp[:, :], V(bass.AP(tensor=c16.th, offset=0, ap=[[0, 128], [1, NH]]), c16[0:1, :].regs), allow_slow_non_contiguous=True)
        for h in range(NH):
            bs = bstage[rot("bs", 2)]
            for jj, off in ((0, 129), (1, 1)):
                src = V(bass.AP(tensor=ext.th, offset=h * 388 + off, ap=[[1, 128], [1, 128]]),
                        [("D:ext", h * 388, (h + 1) * 388)])
                S.dma("sp", bs[:, jj, :], src)
            vcopy("dve", biasT[:, h, 0, :], mask_sb[:, 0, :])
            stt(biasT[:, h, 2, :], bs[:, 0, :], cbp[:, h:h + 1], mask_sb[:, 2, :], ALU.subtract, ALU.add)
            stt(biasT[:, h, 3, :], bs[:, 1, :], cbp[:, h:h + 1], mask_sb[:, 3, :], ALU.subtract, ALU.add)

    def S_view_rows_unused():
        pass

    JJ = [0, 1, 1, 2, 3]

    def conv_ops(l, td):
        T = td.T
        for p in range(4):
            for j in range(CW):
                for c in (2 * p, 2 * p + 1):
                    wcol = vec[:, 88 + j * 8 + c:88 + j * 8 + c + 1]
                    for (tok0, n, ucol) in td.segs:
                        src = uT[:, c, ucol - 30 + j:ucol - 30 + j + n]
                        dst = yc[:, c, tok0:tok0 + n]
                        if j == 0:
                            ts(dst, src, wcol, vec[:, 64 + c:65 + c], ALU.mult, ALU.add)
                        else:
                            stt(dst, src, wcol, dst, ALU.mult, ALU.add)
                        yield

    def attention(l, td, filler):
        units = td.units
        sched = []
        for u in units:
            for h in range(NH):
                sched.append((u, h))
        state = {}

        def emit_S(u, h):
            nq, qcol, blks, yrow = u
            c = h // 2
            p0 = (h % 2) * 64
            stp = stps[rot("st", 2)]
            sbase = 0
            for j, rb in blks:
                o = sbase + j * 128
                nb_ = JJ[j] == 1
                mm(stp[:, o:o + nq], kring[p0:p0 + 64, c, rb * 128:rb * 128 + 128], qT[p0:p0 + 64, c, qcol:qcol + nq], True, nb_)
                if not nb_:
                    mm(stp[:, o:o + nq], Jb[:, :], biasT[:, h, JJ[j], 0:nq], False, True)
            j0 = blks[0][0]
            nb = len(blks)
            pt = PT[rot("pt", 2)]
            src = stp.view(stp.th[:, sbase + j0 * 128:sbase + (j0 + nb) * 128].rearrange("p (j q) -> p j q", q=128)[:, :, 0:nq])
            dst = pt.view(pt.th[:, j0 * 128:(j0 + nb) * 128].rearrange("p (j q) -> p j q", q=128)[:, :, 0:nq])
            act(dst, src, AF.Exp)
            state[(id(u), h)] = pt

        def emit_PV(u, h):
            nq, qcol, blks, yrow = u
            pt = state.pop((id(u), h))
            pi = rot("pv", 2)
            pvp = pvps[pi]
            po = 0
            for i, (j, rb) in enumerate(blks):
                mm(pvp[0:nq, po:po + 65], pt[:, j * 128:j * 128 + nq], vring[:, rb, h, :], i == 0, i == len(blks) - 1)
            r = rcp[0:nq, pi:pi + 1]
            S.op("dve", lambda e: e.reciprocal(out=r.ap, in_=pvp[0:nq, po + 64:po + 65].ap), [pvp[0:nq, po + 64:po + 65]], [r])
            act(ya[0:nq, h * 64:(h + 1) * 64], pvp[0:nq, po:po + 64], AF.Copy, scale=r)
            for _ in range(4):
                next(filler, None)
            if h == NH - 1:
                for b4 in range(2):
                    trp = acc_attn()
                    for cc in range(4):
                        c = 4 * b4 + cc
                        transpose(trp[:, cc * 128:cc * 128 + nq], ya[0:nq, c * 128:(c + 1) * 128], cst_sb[0:nq, 0, 0:nq])
                    src = trp.view(trp.th[:, :].rearrange("p (c q) -> p c q", q=128)[:, :, 0:nq])
                    vcopy("act", yaT[:, 4 * b4:4 * b4 + 4, qcol:qcol + nq], src)

        for i, (u, h) in enumerate(sched):
            emit_S(u, h)
            if i >= 1:
                emit_PV(*sched[i - 1])
        emit_PV(*sched[-1])
        for _ in filler:
            pass

    def tile(l, td, nxt=None):
        T = td.T
        last_layer = (l == NL - 1)
        if l == 0:
            for (src_rows, n, col0) in td.xin:
                S.dma("sp", xstage[0:n, :], src_rows)
                for b4 in range(4):
                    trp = acc()
                    for cc in range(4):
                        c = 4 * b4 + cc
                        transpose(trp[:, cc * 128:cc * 128 + n], xstage[0:n, c * 128:(c + 1) * 128], cst_sb[0:n, 0, 0:n])
                    src = trp.view(trp.th[:, :].rearrange("p (c q) -> p c q", q=128)[:, :, 0:n])
                    vcopy("act" if b4 % 2 else "dve", xT[:, 4 * b4:4 * b4 + 4, col0:col0 + n], src)
        elif not getattr(td, "prefetched", False):
            for g in range(4):
                S.dma("sp", xT[:, 4 * g:4 * g + 4, 0:T], xscr[td.tid, :, 4 * g:4 * g + 4, 0:T])
        ckpt('s0')
        rmsnorm(0, T)
        ckpt('s1')
        if td.kind == "prompt":
            if td.ti == 0:
                memset("dve", uT[:, :, 0:32], 0.0)
            else:
                vcopy("dve", uT[:, :, 0:32], upref[:, :, :])
        else:
            for s in range(2):
                S.dma("sp", pstage[0:30, :], scv[l, s, :, :])
                trp = acc()
                for c in range(8):
                    transpose(trp[:, c * 32:c * 32 + 30], pstage[0:30, c * 128:(c + 1) * 128], cst_sb[0:30, 0, 0:30])
                src = trp.view(trp.th[:, 0:256].rearrange("p (c q) -> p c q", q=32)[:, :, 0:30])
                vcopy("dve", uT[:, :, 96 * s + 2:96 * s + 32], src)
        WI = w_in
        conv_gen = conv_ops(l, td)
        for i in range(4):
            if i >= 1:
                for _ in range(16):
                    next(conv_gen, None)
            sl, _ = wload(WI[l, :, 256 * i:256 * i + 256], NCH, 256)
            for cc in range(2):
                c = 2 * i + cc
                ps = acc()
                for k in range(NCH):
                    mm(ps[:, 0:T], wv(sl, NCH, 256, k, cc * 128, cc * 128 + 128), hT[:, k, 0:T], k == 0, k == NCH - 1)
                for (tok0, n, ucol) in td.segs:
                    vcopy("act", uT[:, c, ucol:ucol + n], ps[:, tok0:tok0 + n])
            if td.kv_out:
                col0, n = td.ugroup
                ps = acc()
                for k in range(NCH):
                    mm(ps[0:n, 0:256], hT[:, k, col0:col0 + n], wv(sl, NCH, 256, k, 0, 256), k == 0, k == NCH - 1)
                vcopy("act", a_tm[0:n, 256 * i:256 * i + 256], ps[0:n, 0:256])
            sl, _ = wload(WI[l, :, 1024 + 256 * i:1024 + 256 * i + 256], NCH, 256)
            for cc in range(2):
                c = 2 * i + cc
                ps = acc()
                for k in range(NCH):
                    mm(ps[:, 0:T], wv(sl, NCH, 256, k, cc * 128, cc * 128 + 128), hT[:, k, 0:T], k == 0, k == NCH - 1)
                t = tf[rot("tf", 4)]
                act(t[:, 0:T], ps[:, 0:T], AF.Sigmoid)
                for (tok0, n, ucol) in td.segs:
                    tt(uT[:, c, ucol:ucol + n], uT[:, c, ucol:ucol + n], t[:, tok0:tok0 + n], ALU.mult)
            if td.kv_out:
                col0, n = td.ugroup
                ps = acc()
                for k in range(NCH):
                    mm(ps[0:n, 0:256], hT[:, k, col0:col0 + n], wv(sl, NCH, 256, k, 0, 256), k == 0, k == NCH - 1)
                act(sigb_tm[0:n, :], ps[0:n, 0:256], AF.Sigmoid)
                tt(a_tm[0:n, 256 * i:256 * i + 256], a_tm[0:n, 256 * i:256 * i + 256], sigb_tm[0:n, :], ALU.mult)
        if td.kv_out:
            for (r0, dst) in td.uout:
                S.dma("sp", dst(l), a_tm[r0:r0 + 30, :])
        if td.kind == "prompt" and td.ti < 3:
            vcopy("dve", upref[:, :, :], uT[:, :, 512:544])
        ckpt('s2ab')
        for i in range(4):
            sl, _ = wload(WI[l, :, 2048 + 256 * i:2048 + 256 * i + 256], NCH, 256)
            for cc in range(2):
                c = 2 * i + cc
                ps = acc()
                for k in range(NCH):
                    mm(ps[:, 0:T], wv(sl, NCH, 256, k, cc * 128, cc * 128 + 128), hT[:, k, 0:T], k == 0, k == NCH - 1)
                act(qT[:, c, 0:T], ps[:, 0:T], AF.Copy, scale=0.125)
                for _ in range(5):
                    next(conv_gen, None)
        for i in range(4):
            sl, _ = wload(WI[l, :, 3072 + 256 * i:3072 + 256 * i + 256], NCH, 256)
            for cc in range(2):
                c = 2 * i + cc
                ps = acc()
                for k in range(NCH):
                    mm(ps[:, 0:T], wv(sl, NCH, 256, k, cc * 128, cc * 128 + 128), hT[:, k, 0:T], k == 0, k == NCH - 1)
                for (col0, n, rb, rcol, _o) in td.kgroups:
                    vcopy("act", kring[:, c, rb * 128 + rcol:rb * 128 + rcol + n], ps[:, col0:col0 + n])
                for _ in range(5):
                    next(conv_gen, None)
            if td.kv_out:
                for (col0, n, rb, rcol, outf) in td.kgroups:
                    ps = acc()
                    for k in range(NCH):
                        mm(ps[0:n, 0:256], hT[:, k, col0:col0 + n], wv(sl, NCH, 256, k, 0, 256), k == 0, k == NCH - 1)
                    t = tmk[rot("tmk", 2)]
                    vcopy("act", t[0:n, :], ps[0:n, 0:256])
                    S.dma("sp", outf(nkp if td.kind == "prompt" else nks, l, 256 * i), t[0:n, :])
        for i in range(4):
            sl, _ = wload(WI[l, :, 4096 + 256 * i:4096 + 256 * i + 256], NCH, 256)
            for (col0, n, rb, rcol, outf) in td.kgroups:
                ps = acc()
                for k in range(NCH):
                    mm(ps[0:n, 0:256], hT[:, k, col0:col0 + n], wv(sl, NCH, 256, k, 0, 256), k == 0, k == NCH - 1)
                dst = vring.view(vring.th[rcol:rcol + n, rb, 4 * i:4 * i + 4, 0:64])
                src = ps.view(ps.th[0:n, 0:256].rearrange("p (h d) -> p h d", d=64))
                vcopy("act", dst, src)
                if td.kv_out:
                    t = tmv[rot("tmv", 2)]
                    vcopy("act", t[0:n, :], ps[0:n, 0:256])
                    S.dma("sp", outf(nvp if td.kind == "prompt" else nvs, l, 256 * i), t[0:n, :])
                for _ in range(3):
                    next(conv_gen, None)
        ckpt('s2')
        attention(l, td, conv_gen)
        ckpt('s3')
        ps1 = acc()
        ps2 = acc()
        for c in range(8):
            mm(ps1[:, 0:T], onesf, yc[:, c, 0:T], c == 0, c == 7)
        for c in range(8):
            t = tf[rot("tf", 4)]
            act(t[:, 0:T], yc[:, c, 0:T], AF.Square)
            mm(ps2[:, 0:T], onesf, t[:, 0:T], c == 0, c == 7)
        mean = tf[rot("tf", 4)]
        var = tf[rot("tf", 4)]
        ts(mean[:, 0:T], ps1[:, 0:T], 1.0 / DC, None, ALU.mult)
        tt(var[:, 0:T], mean[:, 0:T], mean[:, 0:T], ALU.mult)
        stt(var[:, 0:T], ps2[:, 0:T], 1.0 / DC, var[:, 0:T], ALU.mult, ALU.subtract)
        act(var[:, 0:T], var[:, 0:T], AF.Sqrt, bias=epsb[:, 0:1], scale=1.0)
        S.op("dve", lambda e: e.reciprocal(out=var[:, 0:T].ap, in_=var[:, 0:T].ap), [var[:, 0:T]], [var[:, 0:T]])
        for c in range(8):
            tt(yc[:, c, 0:T], yc[:, c, 0:T], mean[:, 0:T], ALU.subtract)
            tt(yc[:, c, 0:T], yc[:, c, 0:T], var[:, 0:T], ALU.mult)
            act(ycT[:, c, 0:T], yc[:, c, 0:T], AF.Silu, bias=vec[:, 80 + c:81 + c], scale=vec[:, 72 + c:73 + c])
        ckpt('s4')
        for g in range(8):
            c0 = 256 * g
            sgc = []
            sl, _ = wload(w_gate[l, :, c0:c0 + 256], NCH, 256)
            for cc in range(2):
                ps = acc()
                for k in range(NCH):
                    mm(ps[:, 0:T], wv(sl, NCH, 256, k, cc * 128, cc * 128 + 128), hT[:, k, 0:T], k == 0, k == NCH - 1)
                t = tf[rot("tf", 4)]
                act(t[:, 0:T], ps[:, 0:T], AF.Sigmoid, bias=vec[:, 32 + 2 * g + cc:33 + 2 * g + cc])
                sgc.append(t)
            sl, _ = wload(w_co[l, :, c0:c0 + 256], 8, 256)
            for cc in range(2):
                ps = acc()
                for k in range(8):
                    mm(ps[:, 0:T], wv(sl, 8, 256, k, cc * 128, cc * 128 + 128), ycT[:, k, 0:T], k == 0, k == 7)
                tt(sgc[cc][:, 0:T], ps[:, 0:T], sgc[cc][:, 0:T], ALU.mult)
            sga = []
            sl, _ = wload(w_gate[l, :, D + c0:D + c0 + 256], NCH, 256)
            for cc in range(2):
                ps = acc()
                for k in range(NCH):
                    mm(ps[:, 0:T], wv(sl, NCH, 256, k, cc * 128, cc * 128 + 128), hT[:, k, 0:T], k == 0, k == NCH - 1)
                t = tf[rot("tf", 4)]
                act(t[:, 0:T], ps[:, 0:T], AF.Sigmoid, bias=vec[:, 48 + 2 * g + cc:49 + 2 * g + cc])
                sga.append(t)
            sl, _ = wload(w_o[l, :, c0:c0 + 256], 8, 256)
            for cc in range(2):
                ps = acc()
                for k in range(8):
                    mm(ps[:, 0:T], wv(sl, 8, 256, k, cc * 128, cc * 128 + 128), yaT[:, k, 0:T], k == 0, k == 7)
                tt(sga[cc][:, 0:T], ps[:, 0:T], sga[cc][:, 0:T], ALU.mult)
                tt(mT[:, 2 * g + cc, 0:T], sgc[cc][:, 0:T], sga[cc][:, 0:T], ALU.add)
        ckpt('s5')
        for g in range(8):
            sl, _ = wload(w_out[l, :, 256 * g:256 * g + 256], NCH, 256)
            for cc in range(2):
                oc = 2 * g + cc
                ps = acc()
                for k in range(NCH):
                    mm(ps[:, 0:T], wv(sl, NCH, 256, k, cc * 128, cc * 128 + 128), mT[:, k, 0:T], k == 0, k == NCH - 1)
                tt(xT[:, oc, 0:T], xT[:, oc, 0:T], ps[:, 0:T], ALU.add)
        ckpt('s6')
        rmsnorm(16, T)
        for g in range(NFF // 2):
            sg = []
            sl, _ = wload(w_up[l, :, 256 * g:256 * g + 256], NCH, 256)
            for cc in range(2):
                ps = acc()
                for k in range(NCH):
                    mm(ps[:, 0:T], wv(sl, NCH, 256, k, cc * 128, cc * 128 + 128), hT[:, k, 0:T], k == 0, k == NCH - 1)
                t = tf[rot("tf", 4)]
                act(t[:, 0:T], ps[:, 0:T], AF.Silu)
                sg.append(t)
            sl, _ = wload(w_up[l, :, DFF + 256 * g:DFF + 256 * g + 256], NCH, 256)
            for cc in range(2):
                ps = acc()
                for k in range(NCH):
                    mm(ps[:, 0:T], wv(sl, NCH, 256, k, cc * 128, cc * 128 + 128), hT[:, k, 0:T], k == 0, k == NCH - 1)
                tt(fT[:, 2 * g + cc, 0:T], ps[:, 0:T], sg[cc][:, 0:T], ALU.mult)
        for oc in range(NCH):
            ps = acc()
            for half in range(2):
                sl, _ = wload(w_down[l, half * 2816:(half + 1) * 2816, oc * 128:(oc + 1) * 128], 22, 128)
                for k in range(22):
                    kk = half * 22 + k
                    mm(ps[:, 0:T], wv(sl, 22, 128, k, 0, 128), fT[:, kk, 0:T], kk == 0, kk == NFF - 1)
            tt(xT[:, oc, 0:T], xT[:, oc, 0:T], ps[:, 0:T], ALU.add)
            if oc % 4 == 3 and not last_layer:
                g = oc // 4
                S.dma("sp", xscr[td.tid, :, 4 * g:4 * g + 4, 0:T], xT[:, 4 * g:4 * g + 4, 0:T])
                if nxt is not None:
                    S.dma("sp", xT[:, 4 * g:4 * g + 4, 0:nxt.T], xscr[nxt.tid, :, 4 * g:4 * g + 4, 0:nxt.T])
                    nxt.prefetched = True
        ckpt('s7')
        if not last_layer:
            pass
        elif do_final:
            ps = acc()
            for c in range(NCH):
                s = sq[rot("sq", 2)]
                act(s[:, 0:T], xT[:, c, 0:T], AF.Square)
                mm(ps[:, 0:T], onesb[:, :], s[:, 0:T], c == 0, c == NCH - 1)
            act(rstd[:, 0:T], ps[:, 0:T], AF.Sqrt, bias=epsb[:, 0:1], scale=1.0 / D)
            S.op("dve", lambda e: e.reciprocal(out=rstd[:, 0:T].ap, in_=rstd[:, 0:T].ap), [rstd[:, 0:T]], [rstd[:, 0:T]])
            for (dst_rows, n, col0) in td.yout:
                for b4 in range(4):
                    trp = acc()
                    for cc in range(4):
                        c = 4 * b4 + cc
                        t = tf[rot("tf", 4)]
                        stt(t[:, 0:n], xT[:, c, col0:col0 + n], vecf[:, c:c + 1], rstd[:, col0:col0 + n], ALU.mult, ALU.mult)
                        transpose(trp[0:n, cc * 128:(cc + 1) * 128], t[:, 0:n], ident)
                    vcopy("act", xstage[0:n, b4 * 512:(b4 + 1) * 512], trp[0:n, :])
                S.dma("sp", dst_rows, xstage[0:n, :])

    def prompt_td(seq, ti):
        td = TileDesc()
        td.kind = "prompt"
        td.T = 512
        td.ti = ti
        td.tid = seq * 4 + ti
        td.segs = [(0, 512, 32)]
        td.kv_out = (ti == 3)
        td.ugroup = (384, 128)
        td.uout = [(98, lambda l: ncp[l, seq, :, :])]
        td.kgroups = []
        for tb in range(4):
            rb = (4 * ti + tb) % 8

            def outf(buf, l, c0, tb=tb):
                return buf[l, seq, tb * 128:(tb + 1) * 128, c0:c0 + 256]
            td.kgroups.append((tb * 128, 128, rb, 0, outf))
        td.units = []
        for p in range(4):
            blks = []
            for j in range(5):
                gb = 4 * ti + p - 4 + j
                if gb >= 0:
                    blks.append((j, gb % 8))
            td.units.append((128, p * 128, blks, 0))
        t0 = ti * 512
        td.xin = [(xp[seq, t0 + tb * 128:t0 + (tb + 1) * 128, :], 128, tb * 128) for tb in range(4)]
        td.yout = [(yp[seq, t0 + tb * 128:t0 + (tb + 1) * 128, :], 128, tb * 128) for tb in range(4)]
        return td

    def sample_td():
        td = TileDesc()
        td.kind = "sample"
        td.T = 128
        td.ti = 0
        td.tid = 8
        td.segs = [(0, 64, 32), (64, 64, 128)]
        td.kv_out = True
        td.ugroup = (0, 128)
        td.uout = [(34, lambda l: ncs[l, 0, :, :]), (98, lambda l: ncs[l, 1, :, :])]
        td.kgroups = []
        for s in range(2):
            def outf(buf, l, c0, s=s):
                return buf[l, s, 448:512, c0:c0 + 256]
            td.kgroups.append((s * 64, 64, 5 * s + 4, 0, outf))
        td.units = []
        for s in range(2):
            td.units.append((64, s * 64, [(j, 5 * s + j) for j in range(5)], 0))
        td.xin = [(xsm[s, :, :], 64, s * 64) for s in range(2)]
        td.yout = [(ysm[s, :, :], 64, s * 64) for s in range(2)]
        return td

    def sample_cache(l):
        for s in range(2):
            S.dma("sp", nks[l, s, 0:448, :], ck[l, s, 64:512, :])
            S.dma("sp", nvs[l, s, 0:448, :], cv[l, s, 64:512, :])
            for b in range(4):
                dst = vring.view(vring.th[:, 5 * s + b, :, 0:64])
                srcv = cv.view(cv.root[l, s, b * 128:(b + 1) * 128, :].rearrange("p (h d) -> p h d", d=64))
                S.dma("pool", dst, srcv)
            for b in range(4):
                S.dma("sp", kstage[:, :], ck[l, s, b * 128:(b + 1) * 128, :])
                for b4 in range(2):
                    trp = acc()
                    for cc in range(4):
                        c = 4 * b4 + cc
                        transpose(trp[:, cc * 128:(cc + 1) * 128], kstage[:, c * 128:(c + 1) * 128], ident)
                    src = trp.view(trp.th[:, :].rearrange("p (c q) -> p c q", q=128))
                    vcopy("act" if b4 % 2 else "dve", kring[:, 4 * b4:4 * b4 + 4, (5 * s + b) * 128:(5 * s + b + 1) * 128], src)

    tds_all = [[prompt_td(seq, ti) for seq in range(2) for ti in range(4)] + [sample_td()] for _ in range(NL)]
    try:
        ckpt("init")
        for l in range(NL):
            layer_setup(l)
            ckpt("setup")
            tds = tds_all[l]
            for i, td in enumerate(tds):
                if td.kind == "sample":
                    sample_cache(l)
                    ckpt("scache")
                wstate["b"] = 0
                wstate["first"] = (i == 0)
                if i + 1 < len(tds):
                    nxt, nxt_l = tds[i + 1], l
                elif l + 1 < NL:
                    nxt, nxt_l = tds_all[l + 1][0], l + 1
                else:
                    nxt, nxt_l = None, 0
                tile(l, td, nxt if nxt_l > 0 else None)
                ckpt("tile%d" % i)
                ckpt("L%dtile%d" % (l, i))
    except _Stop:
        pass
    S.emit()
    return nc, S


def S_view_rows(buf, l, nrows):
    return buf.view(buf.root[l:l + 1, :].rearrange("o (r p) -> (o r) p", p=128))


def make_consts():
    cst = np.zeros((128, 3, 128), np.float32)
    cst[:, 0, :] = np.eye(128, dtype=np.float32)
    cst[:, 1, :] = np.eye(128, dtype=np.float32)[::-1]
    cst[:, 2, :] = 1.0
    NEG = -30000.0
    m = np.zeros((128, 4, 128), np.float32)
    for kp in range(128):
        k = 127 - kp
        for q in range(128):
            cq = q // 64
            kc = k // 64
            if (-8 + kc) < (cq - 8):
                m[kp, 0, q] = NEG
            if kc > cq:
                m[kp, 3, q] = NEG
    return cst, m


_CACHE = {}
WNAMES = ["norm_mix", "w_in", "conv_w", "conv_b", "ln_g", "ln_b", "w_conv_out", "rel_bias", "w_o", "w_gate",
          "b_gate", "w_out", "norm_ffn", "w_up", "w_down"]


def run(inputs, NL=DEPTH, cores=8, do_final=True, trace=False, stop=None):
    key = (NL, do_final, stop)
    if key not in _CACHE:
        _CACHE[key] = build(NL, do_final, stop)[0]
    nc = _CACHE[key]
    cst, m = make_consts()
    f = lambda a: np.ascontiguousarray(np.asarray(a, dtype=np.float32))
    shared = {n: f(inputs[n][:NL]) for n in WNAMES}
    shared["norm_final"] = f(inputs["norm_final"]).reshape(1, D)
    shared["cst"] = cst
    shared["maskc"] = m
    in_maps = []
    for c in range(cores):
        d = dict(shared)
        d["xp"] = f(inputs["x_prompt"][2 * c:2 * c + 2])
        d["xsm"] = f(inputs["x_sample"][2 * c:2 * c + 2])
        d["ck"] = f(inputs["cache_k"][:NL, 2 * c:2 * c + 2]).reshape(NL, 2, 512, DC)
        d["cv"] = f(inputs["cache_v"][:NL, 2 * c:2 * c + 2]).reshape(NL, 2, 512, DC)
        d["scv"] = f(inputs["state_conv"][:NL, 2 * c:2 * c + 2])
        in_maps.append(d)
    res = run_bass_kernel_spmd(nc, in_maps, core_ids=list(range(cores)), trace=trace)
    R = res.results
    cat = lambda name, ax: np.concatenate([np.asarray(r[name]) for r in R], axis=ax)
    nb = 2 * cores
    outs = (
        cat("yp", 0),
        cat("ysm", 0),
        cat("nkp", 1).reshape(NL, nb, 512, NH, 64),
        cat("nvp", 1).reshape(NL, nb, 512, NH, 64),
        cat("ncp", 1),
        cat("nks", 1).reshape(NL, nb, 512, NH, 64),
        cat("nvs", 1).reshape(NL, nb, 512, NH, 64),
        cat("ncs", 1),
    )
    return outs, res


def kernel(**inputs):
    outs, _ = run(inputs)
    return outs
```
